# Optimizing a Trainium2 kernel written in Bass

```python
import jax, jax.numpy as jnp
from jax import lax
import numpy as np

D_MODEL = 1024
BATCH = 8
SEQ = 2048
DEPTH = 2
DEC_BATCH = 128
DEC_SEQ = 1
PAST_LEN = 16384
PAGE_SIZE = 128

N_MIXERS = 2
N_ML = (DEPTH + 1) // 2
N_CV = DEPTH // 2
ML_HEADS = 8
ML_DQK = D_MODEL // (2 * ML_HEADS)
ML_DV = D_MODEL // ML_HEADS
ML_HQ = ML_HEADS * ML_DQK
ML_HV = ML_HEADS * ML_DV
ML_IN = 2 * ML_HQ + 2 * ML_HV + 2 * ML_HEADS
ML_CHUNK = 64
GATE_SOFTCAP = 15.0
CONV_W = 3
D_FF = 2816
N_ADA = 9
EPS = 1e-6

kernel_name = 'hybrid_mlstm_shortconv_macaron_decoder_step'


def rmsnorm(x, g):
    xf = x.astype(jnp.float32)
    r = lax.rsqrt(jnp.mean(xf * xf, axis=-1, keepdims=True) + EPS)
    return (xf * r * g.astype(jnp.float32)).astype(x.dtype)


def swiglu(h, w_gate, w_up, w_down):
    return (jax.nn.silu(h @ w_gate) * (h @ w_up)) @ w_down


def mlstm_chunk_step(carry, xs):
    C, n, m = carry
    q, k, v, ig, lf = xs
    L = q.shape[2]
    b = jnp.cumsum(lf, axis=-1)
    causal = jnp.tril(jnp.ones((L, L), dtype=bool))
    d_log = jnp.where(causal, b[..., :, None] - b[..., None, :] + ig[..., None, :], -jnp.inf)
    st_log = b + m[..., None]
    m_t = jnp.maximum(st_log, jnp.max(d_log, axis=-1))
    w_intra = jnp.exp(d_log - m_t[..., None])
    w_state = jnp.exp(st_log - m_t)
    s = jnp.einsum('bhtd,bhsd->bhts', q, k) * w_intra
    num = w_state[..., None] * jnp.einsum('bhtd,bhde->bhte', q, C) + jnp.einsum('bhts,bhse->bhte', s, v)
    den = w_state * jnp.einsum('bhtd,bhd->bht', q, n) + jnp.sum(s, axis=-1)
    h = num / jnp.maximum(jnp.abs(den), jnp.exp(-m_t))[..., None]
    m_new = m_t[..., -1]
    b_last = b[..., -1]
    w_decay = jnp.exp(b_last + m - m_new)
    w_k = jnp.exp(b_last[..., None] - b + ig - m_new[..., None])
    C_new = w_decay[..., None, None] * C + jnp.einsum('bhs,bhsd,bhse->bhde', w_k, k, v)
    n_new = w_decay[..., None] * n + jnp.einsum('bhs,bhsd->bhd', w_k, k)
    return (C_new, n_new, m_new), h


def mlstm_mixer(h, C0, n0, m0, w_in, b_i, b_f, g_head, w_out):
    B, T, _ = h.shape
    proj = (h @ w_in).astype(jnp.float32)
    q_, k_, v_, o_, i_, f_ = jnp.split(
        proj, [ML_HQ, 2 * ML_HQ, 2 * ML_HQ + ML_HV, 2 * ML_HQ + 2 * ML_HV, 2 * ML_HQ + 2 * ML_HV + ML_HEADS], axis=-1)
    heads = lambda t, d: t.reshape(B, T, ML_HEADS, d).transpose(0, 2, 1, 3)
    q = heads(q_, ML_DQK) * (ML_DQK ** -0.5)
    k = heads(k_, ML_DQK)
    v = heads(v_, ML_DV)
    ig = (GATE_SOFTCAP * jnp.tanh((i_ + b_i.astype(jnp.float32)) / GATE_SOFTCAP)).transpose(0, 2, 1)
    lf = jax.nn.log_sigmoid(f_ + b_f.astype(jnp.float32)).transpose(0, 2, 1)
    L = ML_CHUNK if T % ML_CHUNK == 0 else T
    nc = T // L
    chunks = lambda t: jnp.moveaxis(t.reshape(t.shape[:2] + (nc, L) + t.shape[3:]), 2, 0)
    init = (C0.astype(jnp.float32), n0.astype(jnp.float32), m0.astype(jnp.float32))
    (C, n, m), hs = lax.scan(mlstm_chunk_step, init, (chunks(q), chunks(k), chunks(v), chunks(ig), chunks(lf)))
    hs = hs.transpose(1, 0, 3, 2, 4).reshape(B, T, ML_HEADS, ML_DV)
    hn = hs * lax.rsqrt(jnp.mean(hs * hs, axis=-1, keepdims=True) + EPS) * g_head.astype(jnp.float32)
    y = (jax.nn.sigmoid(o_) * hn.reshape(B, T, ML_HV)).astype(h.dtype) @ w_out
    return y, (C, n, m)


def conv_mixer(h, buf, w_in, conv_w, w_out):
    T = h.shape[1]
    bg, cg, xin = jnp.split(h @ w_in, 3, axis=-1)
    u = cg * xin
    upad = jnp.concatenate([buf.astype(u.dtype), u], axis=1)
    conv = conv_w[0] * upad[:, 0:T]
    for j in range(1, CONV_W):
        conv = conv + conv_w[j] * upad[:, j:j + T]
    y = (bg * conv) @ w_out
    return y, upad[:, T:]


def trunk(x, c, ml_C, ml_n, ml_m, cv_buf, w_ada, b_ada, g_pre, g_post, ffn_wg, ffn_wu, ffn_wd,
          ml_w_in, ml_b_i, ml_b_f, ml_g_head, ml_w_out, cv_w_in, cv_conv_w, cv_w_out):
    B = x.shape[0]
    new_C, new_n, new_m, new_buf = [], [], [], []
    for l in range(DEPTH):
        mod = (jax.nn.silu(c) @ w_ada[l] + b_ada[l]).reshape(B, N_ADA, D_MODEL)[:, :, None, :]
        shift = lambda s: mod[:, 3 * s]
        scale = lambda s: mod[:, 3 * s + 1]
        gate = lambda s: 1.0 + mod[:, 3 * s + 2]
        h = rmsnorm(x, g_pre[l, 0]) * (1.0 + scale(0)) + shift(0)
        x = x + 0.5 * gate(0) * rmsnorm(swiglu(h, ffn_wg[l, 0], ffn_wu[l, 0], ffn_wd[l, 0]), g_post[l, 0])
        h = rmsnorm(x, g_pre[l, 1]) * (1.0 + scale(1)) + shift(1)
        j = l // N_MIXERS
        if l % N_MIXERS == 0:
            out, (Cj, nj, mj) = mlstm_mixer(h, ml_C[j], ml_n[j], ml_m[j], ml_w_in[j], ml_b_i[j], ml_b_f[j],
                                            ml_g_head[j], ml_w_out[j])
            new_C.append(Cj); new_n.append(nj); new_m.append(mj)
        else:
            out, bj = conv_mixer(h, cv_buf[j], cv_w_in[j], cv_conv_w[j], cv_w_out[j])
            new_buf.append(bj)
        x = x + gate(1) * rmsnorm(out, g_post[l, 1])
        h = rmsnorm(x, g_pre[l, 2]) * (1.0 + scale(2)) + shift(2)
        x = x + 0.5 * gate(2) * rmsnorm(swiglu(h, ffn_wg[l, 1], ffn_wu[l, 1], ffn_wd[l, 1]), g_post[l, 2])
    return x, jnp.stack(new_C), jnp.stack(new_n), jnp.stack(new_m), jnp.stack(new_buf)


def setup_inputs(seed: int = 0) -> dict:
    key = jax.random.key(seed)
    ks = jax.random.split(key, 24)
    nrm = lambda k, shape, s=1.0: s * jax.random.normal(k, shape, dtype=jnp.float32)
    D = D_MODEL
    return {
        'x_prompt': nrm(ks[0], (BATCH, SEQ, D)),
        'x_sample': nrm(ks[1], (DEC_BATCH, DEC_SEQ, D)),
        'c_prompt': nrm(ks[2], (BATCH, D)),
        'c_sample': nrm(ks[3], (DEC_BATCH, D)),
        'state_mlstm_C': nrm(ks[4], (N_ML, DEC_BATCH, ML_HEADS, ML_DQK, ML_DV), 0.3),
        'state_mlstm_n': jnp.abs(nrm(ks[5], (N_ML, DEC_BATCH, ML_HEADS, ML_DQK))),
        'state_mlstm_m': nrm(ks[6], (N_ML, DEC_BATCH, ML_HEADS)),
        'state_conv': nrm(ks[7], (N_CV, DEC_BATCH, CONV_W - 1, D)),
        'w_ada': nrm(ks[8], (DEPTH, D, N_ADA * D), 0.5 * D ** -0.5),
        'b_ada': nrm(ks[9], (DEPTH, N_ADA * D), 0.02),
        'g_pre': 1.0 + nrm(ks[10], (DEPTH, 3, D), 0.05),
        'g_post': 1.0 + nrm(ks[11], (DEPTH, 3, D), 0.05),
        'ffn_wg': nrm(ks[12], (DEPTH, 2, D, D_FF), D ** -0.5),
        'ffn_wu': nrm(ks[13], (DEPTH, 2, D, D_FF), D ** -0.5),
        'ffn_wd': nrm(ks[14], (DEPTH, 2, D_FF, D), D_FF ** -0.5),
        'ml_w_in': nrm(ks[15], (N_ML, D, ML_IN), D ** -0.5),
        'ml_b_i': nrm(ks[16], (N_ML, ML_HEADS), 0.1) - 1.0,
        'ml_b_f': 3.0 + nrm(ks[17], (N_ML, ML_HEADS), 0.5),
        'ml_g_head': 1.0 + nrm(ks[18], (N_ML, ML_HEADS, ML_DV), 0.05),
        'ml_w_out': nrm(ks[19], (N_ML, ML_HV, D), ML_HV ** -0.5),
        'cv_w_in': nrm(ks[20], (N_CV, D, 3 * D), D ** -0.5),
        'cv_conv_w': nrm(ks[21], (N_CV, CONV_W, D), CONV_W ** -0.5),
        'cv_w_out': nrm(ks[22], (N_CV, D, D), D ** -0.5),
    }


def reference(x_prompt, x_sample, c_prompt, c_sample, state_mlstm_C, state_mlstm_n, state_mlstm_m, state_conv,
              w_ada, b_ada, g_pre, g_post, ffn_wg, ffn_wu, ffn_wd,
              ml_w_in, ml_b_i, ml_b_f, ml_g_head, ml_w_out, cv_w_in, cv_conv_w, cv_w_out):
    weights = (w_ada, b_ada, g_pre, g_post, ffn_wg, ffn_wu, ffn_wd,
               ml_w_in, ml_b_i, ml_b_f, ml_g_head, ml_w_out, cv_w_in, cv_conv_w, cv_w_out)
    zC = jnp.zeros((N_ML, BATCH, ML_HEADS, ML_DQK, ML_DV), jnp.float32)
    zn = jnp.zeros((N_ML, BATCH, ML_HEADS, ML_DQK), jnp.float32)
    zm = jnp.zeros((N_ML, BATCH, ML_HEADS), jnp.float32)
    zb = jnp.zeros((N_CV, BATCH, CONV_W - 1, D_MODEL), x_prompt.dtype)
    y_prompt, pC, pn, pm, pb = trunk(x_prompt, c_prompt, zC, zn, zm, zb, *weights)
    y_sample, sC, sn, sm, sb = trunk(x_sample, c_sample, state_mlstm_C, state_mlstm_n, state_mlstm_m,
                                     state_conv, *weights)
    return (y_prompt, y_sample, pC, pn, pm, pb, sC, sn, sm, sb)
```

```python
import numpy as np
from contextlib import ExitStack
import concourse.bass as bass
import concourse.mybir as mybir
from concourse.bass_utils import run_bass_kernel_spmd

F32 = mybir.dt.float32
BF16 = mybir.dt.bfloat16
AF = mybir.ActivationFunctionType
ALU = mybir.AluOpType
AX = mybir.AxisListType

NCORES = 8
D = 1024
KC = 8
T = 2048
NS = 16
TT = T + NS
DFF = 2816
FC = 22
H = 8
DQK = 64
DV = 128
MLIN = 3088
EPS = 1e-6
CAP = 15.0
NBLK = T // 128
AUG = 160
NA = 132

DEBUG_TAPS = {}


ALL_BUFS = []


class StopProg(Exception):
    pass


class Buf:
    __slots__ = ("W", "R", "name")

    def __init__(self, name=""):
        self.W = {}
        self.R = {}
        self.name = name
        ALL_BUFS.append(self)


def _merge(dst, src):
    for k, (s, v) in src.items():
        if k not in dst or dst[k][1] < v:
            dst[k] = (s, v)


class KB:
    def __init__(self, nc, es, n_dma_sems=8):
        self.nc = nc
        self.engs = {"pe": nc.tensor, "act": nc.scalar, "dve": nc.vector, "pool": nc.gpsimd, "sp": nc.sync}
        self.sem = {k: es.enter_context(nc.semaphore("s_" + k)) for k in self.engs}
        self.cnt = {k: 0 for k in self.engs}
        self.waited = {k: {} for k in self.engs}
        self.dma_sems = {q: [es.enter_context(nc.semaphore(f"d_{q}{i}")) for i in range(n_dma_sems)] for q in ("sp", "pool")}
        self.dma_i = {"sp": 0, "pool": 0}
        self.stopped = False

    def _wait(self, e, deps):
        eng = self.engs[e]
        wt = self.waited[e]
        for key, (s, v) in deps.items():
            if wt.get(key, 0) < v:
                eng.wait_ge(s, v)
                wt[key] = v

    def op(self, e, fn, r=(), w=()):
        if self.stopped:
            return None
        deps = {}
        for b in r:
            _merge(deps, b.W)
        for b in w:
            _merge(deps, b.W)
            _merge(deps, b.R)
        self._wait(e, deps)
        ins = fn(self.engs[e])
        self.cnt[e] += 1
        ins.then_inc(self.sem[e], 1)
        tok = {e: (self.sem[e], self.cnt[e])}
        for b in r:
            _merge(b.R, tok)
        for b in w:
            _merge(b.W, tok)
        return ins

    def dma(self, q, out, in_, r=(), w=(), **kw):
        if self.stopped:
            return None
        deps = {}
        for b in r:
            _merge(deps, b.W)
        for b in w:
            _merge(deps, b.W)
            _merge(deps, b.R)
        i = self.dma_i[q]
        self.dma_i[q] += 1
        sems = self.dma_sems[q]
        n = len(sems)
        s = sems[i % n]
        val = 16 * (i // n + 1)
        key = ("dma", q, i % n)
        if val > 16:
            deps[key] = (s, val - 16) if (key not in deps or deps[key][1] < val - 16) else deps[key]
        self._wait(q, deps)
        ins = self.engs[q].dma_start(out=out, in_=in_, **kw).then_inc(s, 16)
        tok = {key: (s, val)}
        for b in r:
            _merge(b.R, tok)
        for b in w:
            _merge(b.W, tok)
        return ins

    def finish(self, e, bufs):
        if self.stopped:
            return
        deps = {}
        for b in bufs:
            _merge(deps, b.W)
            _merge(deps, b.R)
        self._wait(e, deps)


def build_program(taps=None):
    taps = taps or {}
    del ALL_BUFS[:]
    MSTOP = taps.get("_mstop", 0)

    KREF = [None]

    def stage(k):
        if MSTOP == k:
            KREF[0].stopped = True
    nc = bass.Bass("TRN2", target_bir_lowering=False)

    def din(name, shape):
        return nc.dram_tensor(name, list(shape), F32, kind="ExternalInput").ap()

    def dout(name, shape):
        return nc.dram_tensor(name, list(shape), F32, kind="ExternalOutput").ap()

    x_p = din("x_p", [T, D])
    x_s = din("x_s", [NS, D])
    c_all = din("c_all", [NS + 1, D])
    st_C = din("st_C", [NS, H, DQK, DV])
    st_n = din("st_n", [NS, H, DQK])
    st_m = din("st_m", [NS, H])
    st_cv = din("st_cv", [NS, 2, D])
    w_ada = din("w_ada", [2, D, 9 * D])
    b_ada = din("b_ada", [2, 9 * D])
    g_pre = din("g_pre", [2, 3, D])
    g_post = din("g_post", [2, 3, D])
    ffn_wg = din("ffn_wg", [2, 2, D, DFF])
    ffn_wu = din("ffn_wu", [2, 2, D, DFF])
    ffn_wd = din("ffn_wd", [2, 2, DFF, D])
    ml_w_in = din("ml_w_in", [1, D, MLIN])
    ml_b_i = din("ml_b_i", [1, H])
    ml_b_f = din("ml_b_f", [1, H])
    ml_g_head = din("ml_g_head", [1, H, DV])
    ml_w_out = din("ml_w_out", [1, D, D])
    cv_w_in = din("cv_w_in", [1, D, 3 * D])
    cv_conv_w = din("cv_conv_w", [1, 3, D])
    cv_w_out = din("cv_w_out", [1, D, D])

    y_p = dout("y_p", [T, D])
    y_s = dout("y_s", [NS, D])
    C_p = dout("C_p", [H, DQK, DV])
    n_p = dout("n_p", [H, DQK])
    m_p = dout("m_p", [H, 1])
    cv_p = dout("cv_p", [2, D])
    C_s = dout("C_s", [NS, H, DQK, DV])
    n_s = dout("n_s", [NS, H, DQK])
    m_s = dout("m_s", [NS, H])
    cv_s = dout("cv_s", [NS, 2, D])
    tap_out = {k: dout("tap_" + k, shp) for k, shp in taps.items() if not k.startswith("_")}

    with ExitStack() as es:
        E = es.enter_context
        K = KB(nc, es)
        KREF[0] = K
        op, dma = K.op, K.dma

        uid = [0]

        def sb(name, shape, dt=F32, stack=None):
            uid[0] += 1
            return (stack or es).enter_context(nc.sbuf_tensor(f"{name}_{uid[0]}", list(shape), dt))

        xT = sb("xT", [128, KC, TT])
        xB = [Buf(f"x{i}") for i in range(5)]
        ABG = sb("ABG", [128, 3, 3, KC, NS + 1])
        ABGb = Buf("ABG")
        vA = sb("vA", [128, 120])
        vB = sb("vB", [128, 120])
        vC = sb("vC", [128, 32])
        vAb, vBb, vCb = Buf("vA"), Buf("vB"), Buf("vC")
        ident = sb("ident", [128, 128])
        identb = sb("identb", [128, 128], BF16)
        onesb = sb("onesb", [128, 128], BF16)
        cmask = sb("cmask", [128, 128])
        eps_t = sb("eps_t", [128, 1])
        cT = sb("cT", [128, KC, NS + 1], BF16)
        constb = Buf("const")
        cTb = Buf("cT")
        ps = E(nc.psum_tensor("ps", [128, 8, 512], F32))
        PB = [Buf(f"ps{i}") for i in range(8)]
        RING_N = 4
        ring = sb("ring", [128, RING_N, 4096], BF16)
        ringB = [Buf(f"ring{i}") for i in range(RING_N)]
        ring_i = [0]

        def ring_load(src_ap, shape):
            i = ring_i[0] % RING_N
            ring_i[0] += 1
            n = int(np.prod(shape))
            assert n <= 4096
            v = ring[:, i, 0:n]
            if len(shape) == 2:
                v = v.rearrange("p (a b) -> p a b", a=shape[0])
            elif len(shape) == 3:
                v = v.rearrange("p (a b c) -> p a b c", a=shape[0], b=shape[1])
            dma("pool", v, src_ap, w=[ringB[i]])
            return v, ringB[i]

        def tap(name, ap, bufs):
            if name in tap_out:
                dma("pool", tap_out[name], ap, r=bufs)

        op("pool", lambda g: g.memset(ident[:], 0.0), w=[constb])
        op("pool", lambda g: g.affine_select(out=ident[:], in_=ident[:], pattern=[[-1, 128]], compare_op=ALU.not_equal, fill=1.0, base=0, channel_multiplier=1), r=[constb], w=[constb])
        op("pool", lambda g: g.tensor_copy(out=identb[:], in_=ident[:]), r=[constb], w=[constb])
        op("pool", lambda g: g.memset(onesb[:], 1.0), w=[constb])
        op("pool", lambda g: g.memset(eps_t[:], EPS), w=[constb])
        op("pool", lambda g: g.memset(cmask[:], 1.0), w=[constb])
        op("pool", lambda g: g.affine_select(out=cmask[:], in_=cmask[:], pattern=[[1, 128]], compare_op=ALU.is_ge, fill=0.0, base=0, channel_multiplier=-1), r=[constb], w=[constb])

        with ExitStack() as s0:
            rowsA = sb("rowsA", [120, 128], stack=s0)
            rowsB = sb("rowsB", [120, 128], stack=s0)
            rowsC = sb("rowsC", [32, 128], stack=s0)
            crow = sb("crow", [NS + 1, D], stack=s0)
            xs = [sb(f"xs{i}", [128, D], stack=s0) for i in range(3)]
            rAb, rBb, rCb, crb = Buf(), Buf(), Buf(), Buf()
            xsb = [Buf() for _ in range(3)]
            dma("sp", crow[:], c_all, w=[crb])
            dma("sp", rowsA[0:72, :], b_ada[0].rearrange("(c p) -> c p", p=128), w=[rAb])
            dma("sp", rowsA[72:120, :], g_pre.rearrange("l s (c p) -> (l s c) p", p=128), w=[rAb])
            dma("sp", rowsB[0:72, :], b_ada[1].rearrange("(c p) -> c p", p=128), w=[rBb])
            dma("sp", rowsB[72:120, :], g_post.rearrange("l s (c p) -> (l s c) p", p=128), w=[rBb])
            dma("sp", rowsC[0:24, :], cv_conv_w[0].rearrange("j (c p) -> (j c) p", p=128), w=[rCb])
            dma("sp", rowsC[24:32, :], ml_g_head[0], w=[rCb])
            for (rows, rb, n, dst, db) in ((rowsA, rAb, 120, vA, vAb), (rowsB, rBb, 120, vB, vBb), (rowsC, rCb, 32, vC, vCb)):
                op("pe", lambda pe, rows=rows, n=n: pe.transpose(ps[:, 0, 0:n], rows[0:n, :], ident[0:n, 0:n]), r=[rb, constb], w=[PB[0]])
                op("dve", lambda v, dst=dst, n=n: v.tensor_copy(out=dst[:, 0:n], in_=ps[:, 0, 0:n]), r=[PB[0]], w=[db])
            op("act", lambda a: a.activation(out=crow[:], in_=crow[:], func=AF.Silu), r=[crb], w=[crb])

            def tr_c(pe):
                for k in range(KC):
                    ins = pe.transpose(ps[:, 1, k * 32:k * 32 + NS + 1], crow[:, k * 128:(k + 1) * 128], ident[0:NS + 1, 0:NS + 1])
                return ins
            op("pe", tr_c, r=[crb, constb], w=[PB[1]])
            op("dve", lambda v: v.tensor_copy(out=cT[:], in_=ps[:, 1, 0:KC * 32].rearrange("p (k c) -> p k c", c=32)[:, :, 0:NS + 1]), r=[PB[1]], w=[cTb])

            for blk in range(NBLK + 1):
                i = blk % 3
                n = 128 if blk < NBLK else NS
                src = x_p[blk * 128:(blk + 1) * 128, :] if blk < NBLK else x_s
                dma("sp", xs[i][0:n, :], src, w=[xsb[i]])
                pb = (2 + 2 * (blk % 3))

                def tr_x(pe, i=i, n=n, pb=pb):
                    for k in range(KC):
                        ins = pe.transpose(ps[:, pb + k // 4, (k % 4) * 128:(k % 4) * 128 + n], xs[i][0:n, k * 128:(k + 1) * 128], ident[0:n, 0:n])
                    return ins
                op("pe", tr_x, r=[xsb[i], constb], w=[PB[pb], PB[pb + 1]])
                xb = xB[blk // 4] if blk < NBLK else xB[4]
                src_ps = ps[:, pb:pb + 2, :].rearrange("p b (k t) -> p (b k) t", t=128)[:, :, 0:n]
                eng = "dve" if blk % 2 == 0 else "act"
                if eng == "dve":
                    op("dve", lambda v, blk=blk, n=n, src_ps=src_ps: v.tensor_copy(out=xT[:, :, blk * 128:blk * 128 + n], in_=src_ps), r=[PB[pb], PB[pb + 1]], w=[xb])
                else:
                    op("act", lambda a, blk=blk, n=n, src_ps=src_ps: a.copy(out=xT[:, :, blk * 128:blk * 128 + n], in_=src_ps), r=[PB[pb], PB[pb + 1]], w=[xb])
            for e_ in ("pe", "act", "dve", "pool", "sp"):
                K.finish(e_, [rAb, rBb, rCb, crb] + xsb)

        tiles = [(0, 512, 0), (512, 512, 1), (1024, 512, 2), (1536, 512, 3), (T, NS, 4)]

        def rstd_from_sq(sqv, sqb, n, r_ap, rb, bank):
            def mm(pe):
                for k in range(KC):
                    ins = pe.matmul(ps[:, bank, 0:n], lhsT=onesb[:], rhs=sqv[:, k, :], start=(k == 0), stop=(k == KC - 1))
                return ins
            op("pe", mm, r=(list(sqb) if isinstance(sqb, (list, tuple)) else [sqb]) + [constb], w=[PB[bank]])
            op("act", lambda a: a.activation(out=r_ap, in_=ps[:, bank, 0:n], func=AF.Sqrt, bias=eps_t[:, 0:1], scale=1.0 / D), r=[PB[bank], constb], w=[rb])
            op("dve", lambda v: v.reciprocal(out=r_ap, in_=r_ap), r=[rb], w=[rb])

        def pre_norm(sub, h_ap_fn, hb_fn, tl, S):
            for ti in tl:
                c0, n, xi = tiles[ti]
                j = ti % 2
                r_ap, rb = S["r"][j][:, 0:n], S["rb"][j]
                hv = h_ap_fn(ti)
                hw = hb_fn(ti)
                op("act", lambda a, c0=c0, n=n, hv=hv: a.activation(out=hv, in_=xT[:, :, c0:c0 + n], func=AF.Square), r=[xB[xi]], w=hw)
                rstd_from_sq(hv, hw, n, r_ap, rb, 7)
                if ti < 4:
                    for k in range(KC):
                        tm, tmb = S["tmp"][k % 2], S["tmpb"][k % 2]
                        op("dve", lambda v, k=k, c0=c0, n=n, tm=tm, r_ap=r_ap: v.tensor_tensor(out=tm[:, 0:n], in0=xT[:, k, c0:c0 + n], in1=r_ap, op=ALU.mult), r=[xB[xi], rb], w=[tmb])
                        op("act", lambda a, k=k, n=n, tm=tm, hv=hv: a.activation(out=hv[:, k, :], in_=tm[:, 0:n], func=AF.Identity, bias=ABG[:, sub, 1, k, NS:NS + 1], scale=ABG[:, sub, 0, k, NS:NS + 1]), r=[tmb, ABGb], w=hw)
                else:
                    tm, tmb = S["tmp"][0], S["tmpb"][0]
                    tv = tm[:, 0:KC * NS].rearrange("p (k c) -> p k c", c=NS)
                    op("dve", lambda v: v.tensor_tensor(out=tv, in0=xT[:, :, T:TT], in1=r_ap[:, None, :].broadcast_to([128, KC, NS]), op=ALU.mult), r=[xB[4], rb], w=[tmb])
                    op("dve", lambda v: v.tensor_tensor(out=tv, in0=tv, in1=ABG[:, sub, 0, :, 0:NS], op=ALU.mult), r=[tmb, ABGb], w=[tmb])
                    op("dve", lambda v: v.tensor_tensor(out=hv, in0=tv, in1=ABG[:, sub, 1, :, 0:NS], op=ALU.add), r=[tmb, ABGb], w=hw)

        def post_norm_update(sub, out_ap, outb, sq_ap, sqb, ti, S):
            c0, n, xi = tiles[ti]
            j = ti % 2
            r_ap, rb = S["r"][j][:, 0:n], S["rb"][j]
            rstd_from_sq(sq_ap, sqb, n, r_ap, rb, 7)
            if ti < 4:
                for k in range(KC):
                    tm, tmb = S["tmp"][k % 2], S["tmpb"][k % 2]
                    op("dve", lambda v, k=k, n=n, tm=tm: v.tensor_tensor(out=tm[:, 0:n], in0=out_ap[:, k, :], in1=r_ap, op=ALU.mult), r=[outb, rb], w=[tmb])
                    op("dve", lambda v, k=k, c0=c0, n=n, tm=tm: v.scalar_tensor_tensor(out=xT[:, k, c0:c0 + n], in0=tm[:, 0:n], scalar=ABG[:, sub, 2, k, NS:NS + 1], in1=xT[:, k, c0:c0 + n], op0=ALU.mult, op1=ALU.add), r=[tmb, ABGb, xB[xi]], w=[xB[xi]])
            else:
                tm, tmb = S["tmp"][0], S["tmpb"][0]
                tv = tm[:, 0:KC * NS].rearrange("p (k c) -> p k c", c=NS)
                op("dve", lambda v: v.tensor_tensor(out=tv, in0=out_ap, in1=r_ap[:, None, :].broadcast_to([128, KC, NS]), op=ALU.mult), r=[outb, rb], w=[tmb])
                op("dve", lambda v: v.tensor_tensor(out=tv, in0=tv, in1=ABG[:, sub, 2, :, 0:NS], op=ALU.mult), r=[tmb, ABGb], w=[tmb])
                op("dve", lambda v: v.tensor_tensor(out=xT[:, :, T:TT], in0=xT[:, :, T:TT], in1=tv, op=ALU.add), r=[tmb, xB[4]], w=[xB[4]])

        def alloc_scratch(stack):
            S = {}
            S["r"] = [sb(f"r{i}", [128, 512], stack=stack) for i in range(2)]
            S["rb"] = [Buf() for _ in range(2)]
            S["tmp"] = [sb(f"tmp{i}", [128, 512], stack=stack) for i in range(2)]
            S["tmpb"] = [Buf() for _ in range(2)]
            return S

        def scratch_bufs(S):
            return S["rb"] + S["tmpb"]

        def all_engines_finish(bufs):
            for e in ("pe", "act", "dve", "pool", "sp"):
                K.finish(e, bufs)

        mod = sb("mod", [128, 72, NS + 1])
        modb = Buf("mod")

        def adaln_blocks(l, bank):
            vb_cols = (vA, vAb) if l == 0 else (vB, vBb)
            for blk in range(18):
                wv, wb = ring_load(w_ada[l, :, blk * 512:(blk + 1) * 512].rearrange("(k p) n -> p k n", p=128), [KC, 512])

                def mm(pe, wv=wv):
                    for m in range(4):
                        for k in range(KC):
                            ins = pe.matmul(ps[:, bank, m * 32:m * 32 + NS + 1], lhsT=wv[:, k, m * 128:(m + 1) * 128], rhs=cT[:, k, :], start=(k == 0), stop=(k == KC - 1))
                    return ins
                op("pe", mm, r=[wb, cTb], w=[PB[bank]])
                op("dve", lambda v, blk=blk: v.tensor_tensor(
                    out=mod[:, blk * 4:blk * 4 + 4, :],
                    in0=ps[:, bank, 0:128].rearrange("p (m c) -> p m c", c=32)[:, :, 0:NS + 1],
                    in1=vb_cols[0][:, blk * 4:blk * 4 + 4, None].broadcast_to([128, 4, NS + 1]), op=ALU.add),
                    r=[PB[bank], vb_cols[1]], w=[modb])
                yield

        def adaln_abg(l, subs):
            for s in subs:
                gpre = vA[:, 72 + (l * 3 + s) * 8:72 + (l * 3 + s) * 8 + 8]
                gpost = vB[:, 72 + (l * 3 + s) * 8:72 + (l * 3 + s) * 8 + 8]
                sh = mod[:, (3 * s) * 8:(3 * s) * 8 + 8, :]
                sc = mod[:, (3 * s + 1) * 8:(3 * s + 1) * 8 + 8, :]
                gt = mod[:, (3 * s + 2) * 8:(3 * s + 2) * 8 + 8, :]
                gfac = 1.0 if s == 1 else 0.5
                op("dve", lambda v, s=s, sc=sc, gpre=gpre: v.scalar_tensor_tensor(out=ABG[:, s, 0, :, :], in0=sc, scalar=1.0, in1=gpre[:, :, None].broadcast_to([128, KC, NS + 1]), op0=ALU.add, op1=ALU.mult), r=[modb, vAb], w=[ABGb])
                op("dve", lambda v, s=s, sh=sh: v.tensor_copy(out=ABG[:, s, 1, :, :], in_=sh), r=[modb], w=[ABGb])
                op("dve", lambda v, s=s, gt=gt, gfac=gfac: v.tensor_scalar(out=ABG[:, s, 2, :, :], in0=gt, scalar1=1.0, scalar2=gfac, op0=ALU.add, op1=ALU.mult), r=[modb], w=[ABGb])
                op("dve", lambda v, s=s, gpost=gpost: v.tensor_tensor(out=ABG[:, s, 2, :, :], in0=ABG[:, s, 2, :, :], in1=gpost[:, :, None].broadcast_to([128, KC, NS + 1]), op=ALU.mult), r=[ABGb, vBb], w=[ABGb])

        def steps(gen, n):
            for _ in range(n):
                if next(gen, "end") == "end":
                    return

        def ffn(l, fi, sub, bg=None, bg_n=1):
            wg = ffn_wg[l, fi]
            wu = ffn_wu[l, fi]
            wd = ffn_wd[l, fi]
            with ExitStack() as s1:
                S = alloc_scratch(s1)
                GT = 528
                hbufs = [sb(f"hbuf{i}", [128, KC * GT * 2], BF16, stack=s1) for i in range(2)]
                h_vs = [hb_[:, 0:KC * GT].rearrange("p (k t) -> p k t", t=GT) for hb_ in hbufs]
                out_vs = [hb_[:].bitcast(F32).rearrange("p (k t) -> p k t", t=GT) for hb_ in hbufs]
                a_vs = [sb(f"a_v{i}", [128, FC, GT], BF16, stack=s1) for i in range(2)]
                hB = [Buf(), Buf()]
                aB = [Buf(), Buf()]
                sg = [sb(f"sg{i}", [128, 512], stack=s1) for i in range(2)]
                sgb = [Buf() for _ in range(2)]
                groups = [[0], [1], [2], [3, 4]]
                cntA = [0]
                cntB = [0]

                def locs(g):
                    return {ti: (j * 512) for j, ti in enumerate(groups[g])}

                def pre(g):
                    loc = locs(g)
                    h_v = h_vs[g % 2]
                    pre_norm(sub, lambda ti: h_v[:, :, loc[ti]:loc[ti] + tiles[ti][1]], lambda ti: [hB[g % 2]], groups[g], S)
                    if g == 0:
                        tap(f"h_l{l}f{fi}", h_v[:, 0, 0:512], [hB[0]])

                def phaseA(g):
                    loc = locs(g)
                    h_v, a_v = h_vs[g % 2], a_vs[g % 2]
                    for mb in range(6):
                        nm = 4 if mb < 5 else 2
                        wgv, wgb = ring_load(wg[:, mb * 512:mb * 512 + nm * 128].rearrange("(k p) n -> p k n", p=128), [KC, nm * 128])
                        wuv, wub = ring_load(wu[:, mb * 512:mb * 512 + nm * 128].rearrange("(k p) n -> p k n", p=128), [KC, nm * 128])
                        for mi in range(nm):
                            m = mb * 4 + mi
                            for ti in groups[g]:
                                n = tiles[ti][1]
                                lc = loc[ti]
                                bg_, bu_ = (0, 1) if cntA[0] % 2 == 0 else (2, 3)
                                cntA[0] += 1

                                def mm(pe, wv=wgv, bank=bg_, mi=mi, lc=lc, n=n):
                                    for k in range(KC):
                                        ins = pe.matmul(ps[:, bank, 0:n], lhsT=wv[:, k, mi * 128:(mi + 1) * 128], rhs=h_v[:, k, lc:lc + n], start=(k == 0), stop=(k == KC - 1))
                                    return ins
                                op("pe", mm, r=[wgb, hB[g % 2]], w=[PB[bg_]])
                                op("pe", lambda pe, wv=wuv, bank=bu_, mi=mi, lc=lc, n=n: mm(pe, wv, bank, mi, lc, n), r=[wub, hB[g % 2]], w=[PB[bu_]])
                                si = cntA[0] % 2
                                op("act", lambda a, bank=bg_, n=n, si=si: a.activation(out=sg[si][:, 0:n], in_=ps[:, bank, 0:n], func=AF.Silu), r=[PB[bg_]], w=[sgb[si]])
                                op("dve", lambda v, bank=bu_, n=n, si=si, m=m, lc=lc: v.tensor_tensor(out=a_v[:, m, lc:lc + n], in0=sg[si][:, 0:n], in1=ps[:, bank, 0:n], op=ALU.mult), r=[sgb[si], PB[bu_]], w=[aB[g % 2]])
                        if bg is not None:
                            steps(bg, bg_n)

                def phaseB(g):
                    loc = locs(g)
                    a_v, out_v = a_vs[g % 2], out_vs[g % 2]
                    for mo in range(KC):
                        wdv, wdb = ring_load(wd[:, mo * 128:(mo + 1) * 128].rearrange("(k p) n -> p k n", p=128), [FC, 128])
                        for ti in groups[g]:
                            n = tiles[ti][1]
                            lc = loc[ti]
                            bank = 4 + cntB[0] % 3
                            cntB[0] += 1

                            def mm2(pe, wdv=wdv, bank=bank, lc=lc, n=n):
                                for m in range(FC):
                                    ins = pe.matmul(ps[:, bank, 0:n], lhsT=wdv[:, m, :], rhs=a_v[:, m, lc:lc + n], start=(m == 0), stop=(m == FC - 1))
                                return ins
                            op("pe", mm2, r=[wdb, aB[g % 2]], w=[PB[bank]])
                            op("act", lambda a, bank=bank, n=n, mo=mo, lc=lc: a.copy(out=out_v[:, mo, lc:lc + n], in_=ps[:, bank, 0:n]), r=[PB[bank]], w=[hB[g % 2]])

                def post(g):
                    loc = locs(g)
                    a_v, out_v = a_vs[g % 2], out_vs[g % 2]
                    for ti in groups[g]:
                        n = tiles[ti][1]
                        lc = loc[ti]
                        sqv = a_v[:, 0:KC, lc:lc + n]
                        op("act", lambda a, sqv=sqv, n=n, lc=lc: a.activation(out=sqv, in_=out_v[:, :, lc:lc + n], func=AF.Square), r=[hB[g % 2]], w=[aB[g % 2]])
                        post_norm_update(sub, out_v[:, :, lc:lc + n], hB[g % 2], sqv, aB[g % 2], ti, S)
                    if g == 0:
                        tap(f"x_l{l}f{fi}", xT[:, 0, 0:512], [xB[0]])

                NG = len(groups)
                pre(0)
                for g in range(NG):
                    phaseA(g)
                    if g > 0:
                        post(g - 1)
                    phaseB(g)
                    if g + 1 < NG:
                        pre(g + 1)
                post(NG - 1)
                all_engines_finish(scratch_bufs(S) + sgb + hB + aB)

        def out_proj_update(sub, wout, woutb, z_fn, zb_fn, tl, out_ts, out_tb, S):
            for jj, ti in enumerate(tl):
                n = tiles[ti][1]
                zv, zb = z_fn(ti), zb_fn(ti)
                ot, otb = out_ts[jj % len(out_ts)], out_tb[jj % len(out_ts)]
                for mo in range(KC):
                    bank = mo % 6

                    def mm(pe, mo=mo, bank=bank, zv=zv, n=n):
                        for k in range(KC):
                            ins = pe.matmul(ps[:, bank, 0:n], lhsT=wout[:, k, mo * 128:(mo + 1) * 128], rhs=zv[:, k, :], start=(k == 0), stop=(k == KC - 1))
                        return ins
                    op("pe", mm, r=[woutb, zb], w=[PB[bank]])
                    op("act", lambda a, mo=mo, bank=bank, n=n, ot=ot: a.copy(out=ot[:, mo, 0:n], in_=ps[:, bank, 0:n]), r=[PB[bank]], w=[otb])
                op("act", lambda a, zv=zv, ot=ot, n=n: a.activation(out=zv, in_=ot[:, :, 0:n], func=AF.Square), r=[otb], w=[zb])
                post_norm_update(sub, ot[:, :, 0:n], otb, zv, zb, ti, S)

        def conv_mixer(sub):
            with ExitStack() as s1:
                S = alloc_scratch(s1)
                GT = 1040
                hbuf = sb("c_h", [128, KC * GT], BF16, stack=s1)
                h_v = hbuf[:].rearrange("p (k t) -> p k t", t=GT)
                out_t = hbuf[:].bitcast(F32).rearrange("p (k t) -> p k t", t=GT // 2)
                hb = [Buf()] * 3
                z_v = sb("c_z", [128, KC, GT], BF16, stack=s1)
                zb = [Buf() for _ in range(3)]
                wout = sb("c_wout", [128, KC, D], BF16, stack=s1)
                woutb = Buf()
                ubuf = [sb(f"c_u{i}", [128, 2 + GT], stack=s1) for i in range(2)]
                ub = [Buf() for _ in range(2)]
                halo = sb("c_halo", [128, KC, 32], stack=s1)
                halob = Buf()
                cbufT = sb("c_cbufT", [128, KC, 32], stack=s1)
                cbufb = Buf()
                cst = sb("c_cst", [32, D], stack=s1)
                cstb = Buf()
                us_out = sb("c_us", [128, KC, NS], stack=s1)
                usb = Buf()
                cg_t = [sb(f"c_cg{i}", [128, 512], stack=s1) for i in range(2)]
                bg_t = [sb(f"c_bg{i}", [128, 512], stack=s1) for i in range(2)]
                cc_t = [sb(f"c_cc{i}", [128, 512], stack=s1) for i in range(2)]
                cgb = [Buf() for _ in range(2)]
                bgb = [Buf() for _ in range(2)]
                ccb = [Buf() for _ in range(2)]
                orow = sb("c_orow", [NS, D], stack=s1)
                orowb = Buf()
                dma("pool", wout[:], cv_w_out[0].rearrange("(k p) n -> p k n", p=128), w=[woutb])
                op("dve", lambda v: v.memset(halo[:], 0.0), w=[halob])
                dma("sp", cst[:], st_cv.rearrange("i j d -> (i j) d"), w=[cstb])
                dma("sp", cv_s[:, 0, :], st_cv[:, 1, :])

                def tr_cs(pe):
                    for k in range(KC):
                        ins = pe.transpose(ps[:, 6, k * 32:(k + 1) * 32], cst[:, k * 128:(k + 1) * 128], ident[0:32, 0:32])
                    return ins
                op("pe", tr_cs, r=[cstb, constb], w=[PB[6]])
                op("dve", lambda v: v.tensor_copy(out=cbufT[:], in_=ps[:, 6, 0:KC * 32].rearrange("p (k c) -> p k c", c=32)), r=[PB[6]], w=[cbufb])
                cw = lambda j, m: vC[:, j * 8 + m:j * 8 + m + 1]
                cnt = 0
                for g in range(2):
                    tl = [2 * g, 2 * g + 1] + ([4] if g == 1 else [])
                    loc = {ti: (j * 512) for j, ti in enumerate(tl)}
                    pre_norm(sub, lambda ti: h_v[:, :, loc[ti]:loc[ti] + tiles[ti][1]], lambda ti: hb, tl, S)
                    for m in range(KC):
                        si = ring_i[0] % RING_N
                        ring_i[0] += 1
                        wv = ring[:, si, 0:3 * KC * 128].rearrange("p (j k c) -> p k j c", j=3, k=KC)
                        wb = ringB[si]
                        for jj in range(3):
                            dma("pool", ring[:, si, jj * KC * 128:(jj + 1) * KC * 128].rearrange("p (k c) -> p k c", k=KC),
                                cv_w_in[0, :, jj * D + m * 128:jj * D + (m + 1) * 128].rearrange("(k p) c -> p k c", p=128), w=[wb])
                        u, ubb = ubuf[m % 2], ub[m % 2]
                        op("dve", lambda v, u=u, m=m: v.tensor_copy(out=u[:, 0:2], in_=halo[:, m, 0:2]), r=[halob], w=[ubb])
                        for j, ti in enumerate(tl):
                            n = tiles[ti][1]
                            lc = loc[ti]
                            b0 = 3 * (cnt % 2)
                            ci = cnt % 2
                            cnt += 1
                            for jj in range(3):
                                def mm(pe, jj=jj, wv=wv, b0=b0, lc=lc, n=n):
                                    for k in range(KC):
                                        ins = pe.matmul(ps[:, b0 + jj, 0:n], lhsT=wv[:, k, jj, :], rhs=h_v[:, k, lc:lc + n], start=(k == 0), stop=(k == KC - 1))
                                    return ins
                                op("pe", mm, r=[wb, hb[j]], w=[PB[b0 + jj]])
                            op("act", lambda a, ci=ci, b0=b0, n=n: a.copy(out=bg_t[ci][:, 0:n], in_=ps[:, b0, 0:n]), r=[PB[b0]], w=[bgb[ci]])
                            op("act", lambda a, ci=ci, b0=b0, n=n: a.copy(out=cg_t[ci][:, 0:n], in_=ps[:, b0 + 1, 0:n]), r=[PB[b0 + 1]], w=[cgb[ci]])
                            cc, cb = cc_t[ci], ccb[ci]
                            if ti < 4:
                                uc = u[:, 2 + lc:2 + lc + n]
                                op("dve", lambda v, uc=uc, ci=ci, b0=b0, n=n: v.tensor_tensor(out=uc, in0=cg_t[ci][:, 0:n], in1=ps[:, b0 + 2, 0:n], op=ALU.mult), r=[cgb[ci], PB[b0 + 2]], w=[ubb])
                                op("dve", lambda v, uc=uc, cc=cc, m=m, n=n: v.tensor_scalar(out=cc[:, 0:n], in0=uc, scalar1=cw(2, m), scalar2=None, op0=ALU.mult), r=[ubb, vCb], w=[cb])
                                op("dve", lambda v, u=u, cc=cc, m=m, n=n, lc=lc: v.scalar_tensor_tensor(out=cc[:, 0:n], in0=u[:, 1 + lc:1 + lc + n], scalar=cw(1, m), in1=cc[:, 0:n], op0=ALU.mult, op1=ALU.add), r=[ubb, vCb, cb], w=[cb])
                                op("dve", lambda v, u=u, cc=cc, m=m, n=n, lc=lc: v.scalar_tensor_tensor(out=cc[:, 0:n], in0=u[:, lc:lc + n], scalar=cw(0, m), in1=cc[:, 0:n], op0=ALU.mult, op1=ALU.add), r=[ubb, vCb, cb], w=[cb])
                            else:
                                uc = us_out[:, m, :]
                                b01 = cbufT[:, m, :].rearrange("p (i j) -> p i j", j=2)
                                op("dve", lambda v, uc=uc, ci=ci, b0=b0, n=n: v.tensor_tensor(out=uc, in0=cg_t[ci][:, 0:n], in1=ps[:, b0 + 2, 0:n], op=ALU.mult), r=[cgb[ci], PB[b0 + 2]], w=[usb])
                                op("dve", lambda v, uc=uc, cc=cc, m=m, n=n: v.tensor_scalar(out=cc[:, 0:n], in0=uc, scalar1=cw(2, m), scalar2=None, op0=ALU.mult), r=[usb, vCb], w=[cb])
                                op("dve", lambda v, cc=cc, m=m, n=n, b01=b01: v.scalar_tensor_tensor(out=cc[:, 0:n], in0=b01[:, :, 1], scalar=cw(1, m), in1=cc[:, 0:n], op0=ALU.mult, op1=ALU.add), r=[cbufb, vCb, cb], w=[cb])
                                op("dve", lambda v, cc=cc, m=m, n=n, b01=b01: v.scalar_tensor_tensor(out=cc[:, 0:n], in0=b01[:, :, 0], scalar=cw(0, m), in1=cc[:, 0:n], op0=ALU.mult, op1=ALU.add), r=[cbufb, vCb, cb], w=[cb])
                            op("dve", lambda v, cc=cc, ci=ci, m=m, n=n, lc=lc: v.tensor_tensor(out=z_v[:, m, lc:lc + n], in0=cc[:, 0:n], in1=bg_t[ci][:, 0:n], op=ALU.mult), r=[cb, bgb[ci]], w=[zb[j]])
                        op("dve", lambda v, u=u, m=m: v.tensor_copy(out=halo[:, m, 0:2], in_=u[:, 1024:1026]), r=[ubb], w=[halob])
                    if g == 0:
                        tap("cv_z", z_v[:, 0, 0:512], [zb[0]])
                    out_proj_update(sub, wout, woutb, lambda ti: z_v[:, :, loc[ti]:loc[ti] + tiles[ti][1]], lambda ti: zb[tl.index(ti)], tl, [out_t], [hb[0]], S)
                    for e_ in ("pe", "act", "dve"):
                        K.finish(e_, hb)
                def tr_h(pe):
                    for k in range(KC):
                        ins = pe.transpose(ps[0:32, k // 4, (k % 4) * 128:(k % 4 + 1) * 128], halo[:, k, :], ident[:])
                    return ins
                op("pe", tr_h, r=[halob, constb], w=[PB[0], PB[1]])
                op("dve", lambda v: v.tensor_copy(out=orow[0:2, :], in_=ps[0:2, 0:2, :].rearrange("p b t -> p (b t)")), r=[PB[0], PB[1]], w=[orowb])
                dma("sp", cv_p, orow[0:2, :], r=[orowb])

                def tr_u(pe):
                    for k in range(KC):
                        ins = pe.transpose(ps[0:NS, 2 + k // 4, (k % 4) * 128:(k % 4 + 1) * 128], us_out[:, k, :], ident[:])
                    return ins
                op("pe", tr_u, r=[usb, constb], w=[PB[2], PB[3]])
                op("dve", lambda v: v.tensor_copy(out=orow[:, :], in_=ps[0:NS, 2:4, :].rearrange("p b t -> p (b t)")), r=[PB[2], PB[3]], w=[orowb])
                dma("sp", cv_s[:, 1, :], orow[:, :], r=[orowb])
                all_engines_finish(scratch_bufs(S) + hb + zb + [woutb, halob, cbufb, cstb, usb, orowb] + ub + cgb + bgb + ccb)

        def ml_fin_a(n, groups, den_ap, denb, fl_ap, flb, so_ap, sob, yt, ytb, ybf, ybfb, sm8, sm8b):
            den, dn, rdn, ss, t1, scal = [sm8[0:n, i, :] for i in range(6)]
            op("dve", lambda v: v.tensor_tensor(out=dn, in0=den_ap, in1=fl_ap, op=ALU.max), r=[denb, flb], w=[sm8b])
            op("dve", lambda v: v.reciprocal(out=rdn, in_=dn), r=[sm8b], w=[sm8b])
            for (ap, h0, g, bufs) in groups:
                op("act", lambda a, ap=ap, h0=h0, g=g: a.activation(out=yt[0:n, h0:h0 + g, :], in_=ap, func=AF.Square), r=bufs, w=[ytb])
            op("dve", lambda v: v.tensor_reduce(out=ss, in_=yt[0:n, :, :], axis=AX.X, op=ALU.add), r=[ytb], w=[sm8b])
            op("dve", lambda v: v.tensor_tensor(out=t1, in0=rdn, in1=rdn, op=ALU.mult), r=[sm8b], w=[sm8b])
            op("dve", lambda v: v.tensor_tensor(out=t1, in0=t1, in1=ss, op=ALU.mult), r=[sm8b], w=[sm8b])
            op("act", lambda a: a.activation(out=t1, in_=t1, func=AF.Sqrt, bias=eps_t[0:n, 0:1], scale=1.0 / DV), r=[sm8b, constb], w=[sm8b])
            op("dve", lambda v: v.reciprocal(out=t1, in_=t1), r=[sm8b], w=[sm8b])
            op("dve", lambda v: v.tensor_tensor(out=scal, in0=rdn, in1=t1, op=ALU.mult), r=[sm8b], w=[sm8b])
            for (ap, h0, g, bufs) in groups:
                op("dve", lambda v, ap=ap, h0=h0, g=g: v.tensor_tensor(out=yt[0:n, h0:h0 + g, :], in0=ap, in1=scal[:, h0:h0 + g, None].broadcast_to([n, g, DV]), op=ALU.mult), r=bufs + [sm8b], w=[ytb])
            op("dve", lambda v: v.tensor_tensor(out=ybf[0:n, :, :], in0=yt[0:n, :, :], in1=so_ap, op=ALU.mult), r=[ytb, sob], w=[ybfb])

        def ml_fin_b(n, ybf, ybfb, col0, hyw):
            psb7 = ps[:, 7, :].bitcast(BF16)

            def tr(pe):
                for k in range(H):
                    ins = pe.transpose(psb7[:, k * 128:k * 128 + n], ybf[0:n, k, :], identb[0:n, 0:n])
                return ins
            op("pe", tr, r=[ybfb, constb], w=[PB[7]])
            src = psb7.rearrange("p (k t) -> p k t", t=128)[:, :, 0:n]
            op("dve", lambda v: v.tensor_tensor(out=hy_ref[0][:, :, col0:col0 + n], in0=src, in1=vC[:, 24:32, None].broadcast_to([128, H, n]), op=ALU.mult), r=[PB[7], vCb], w=hyw)

        hy_ref = [None]

        def mlstm_mixer(sub):
            w_in = ml_w_in[0]
            with ExitStack() as s1:
                hy = sb("m_hy", [128, KC, TT], BF16, stack=s1)
                hy_ref[0] = hy
                hyB = [Buf() for _ in range(NBLK + 1)]
                voTs = sb("m_voTs", [128, 16, NS], stack=s1)
                qkTs = sb("m_qkTs", [128, 8, NS], BF16, stack=s1)
                gs_rows = sb("m_gsrows", [8, 4, NS], stack=s1)
                gs_tok = sb("m_gstok", [NS, 4, H], stack=s1)
                voTsb, qkTsb, gsrb, gstb = Buf(), Buf(), Buf(), Buf()
                with ExitStack() as s2:
                    S = alloc_scratch(s2)
                    pre_norm(sub, lambda ti: hy[:, :, tiles[ti][0]:tiles[ti][0] + tiles[ti][1]],
                             lambda ti: (hyB[4 * ti:4 * ti + 4] if ti < 4 else [hyB[NBLK]]), [0, 1, 2, 3, 4], S)
                    tap("hm", hy[:, 0, 0:512], hyB[0:4])
                    all_engines_finish(scratch_bufs(S))
                stage(1)
                with ExitStack() as s2:
                    wqk = sb("m_wqk", [128, KC, 1024], BF16, stack=s2)
                    wif = sb("m_wif", [128, KC, 16], BF16, stack=s2)
                    qkT = sb("m_qkT", [128, 8, 512], BF16, stack=s2)
                    ktok = sb("m_ktok", [128, 512], BF16, stack=s2)
                    vaug = sb("m_vaug", [128, H, AUG], BF16, stack=s2)
                    so2 = [sb(f"m_so{i}", [128, H, DV], stack=s2) for i in range(2)]
                    so2b = [Buf(), Buf()]
                    ybf = sb("m_ybf", [128, H, DV], BF16, stack=s2)
                    ybfb = Buf()
                    Sm = sb("m_Sm", [128, H, 128], BF16, stack=s2)
                    yt = sb("m_yt", [128, H, DV], stack=s2)
                    Cst = sb("m_Cst", [128, 4, AUG], stack=s2)
                    Cbf = sb("m_Cbf", [128, 4, AUG], BF16, stack=s2)
                    g_th = sb("m_gth", [8, 512], stack=s2)
                    g_z = sb("m_gz", [8, 512], stack=s2)
                    g_mz = sb("m_gmz", [8, 512], stack=s2)
                    Mp = sb("m_Mp", [8, 640], stack=s2)
                    ones8 = sb("m_ones8", [8, 512], stack=s2)
                    onesf = sb("m_onesf", [8, 128], stack=s2)
                    sm1 = sb("m_sm1", [8, 64], stack=s2)
                    decr = sb("m_decr", [8, 4, 8], stack=s2)
                    ef_tok = sb("m_ef", [128, NBLK, 2, H], stack=s2)
                    decb = sb("m_decb", [128, NBLK, H], stack=s2)
                    sm8 = sb("m_sm8", [128, 6, H], stack=s2)
                    wqkb, wifb, qkTb, ktokb, vaugb, Smb, ytb, Cstb, Cbfb = [Buf() for _ in range(9)]
                    gthb, gzb, gmzb, Mpb, o8b, sm1b, decrb, efb, decbb, sm8b = [Buf() for _ in range(10)]
                    wvo_s = [ring[:, i, :].rearrange("p (k n) -> p k n", k=KC) for i in range(4)]
                    dma("pool", wqk[:], w_in[:, 0:1024].rearrange("(k p) n -> p k n", p=128), w=[wqkb])
                    dma("pool", wif[:], w_in[:, 3072:3088].rearrange("(k p) n -> p k n", p=128), w=[wifb])
                    for i in range(4):
                        dma("pool", wvo_s[i], w_in[:, 1024 + i * 512:1024 + (i + 1) * 512].rearrange("(k p) n -> p k n", p=128), w=[ringB[i]])
                    ring_i[0] = 0
                    dma("sp", sm1[:, 2:3], ml_b_i.rearrange("o h -> h o"), w=[sm1b])
                    dma("sp", sm1[:, 3:4], ml_b_f.rearrange("o h -> h o"), w=[sm1b])
                    dma("sp", sm1[:, 16:32], st_m.rearrange("i h -> h i"), w=[sm1b], allow_slow_non_contiguous=True)
                    op("dve", lambda v: v.tensor_scalar(out=sm1[:, 2:3], in0=sm1[:, 2:3], scalar1=1.0 / CAP, scalar2=None, op0=ALU.mult), r=[sm1b], w=[sm1b])
                    op("dve", lambda v: v.memset(sm1[:, 0:2], 0.0), w=[sm1b])
                    op("dve", lambda v: v.memset(ones8[:], 1.0), w=[o8b])
                    op("dve", lambda v: v.memset(onesf[:], 1.0), w=[o8b])
                    op("dve", lambda v: v.memset(Cst[:], 0.0), w=[Cstb])
                    op("dve", lambda v: v.memset(Cbf[:], 0.0), w=[Cbfb])
                    op("dve", lambda v: v.memset(vaug[:], 0.0), w=[vaugb])
                    op("dve", lambda v: v.memset(Mp[:], 0.0), w=[Mpb])
                    stage(2)

                    def gate_pre(c0, n, hbufs):
                        def mmg(pe, col):
                            for k in range(KC):
                                ins = pe.matmul(ps[0:8, 7, 0:n], lhsT=wif[:, k, col:col + 8], rhs=hy[:, k, c0:c0 + n], start=(k == 0), stop=(k == KC - 1))
                            return ins
                        op("pe", lambda pe: mmg(pe, 0), r=[wifb] + hbufs, w=[PB[7]])
                        op("act", lambda a: a.activation(out=g_th[:, 0:n], in_=ps[0:8, 7, 0:n], func=AF.Tanh, bias=sm1[:, 2:3], scale=1.0 / CAP), r=[PB[7], sm1b], w=[gthb])
                        op("pe", lambda pe: mmg(pe, 8), r=[wifb] + hbufs, w=[PB[7]])
                        op("dve", lambda v: v.tensor_scalar(out=g_z[:, 0:n], in0=ps[0:8, 7, 0:n], scalar1=sm1[:, 3:4], scalar2=None, op0=ALU.add), r=[PB[7], sm1b], w=[gzb])
                        op("dve", lambda v: v.tensor_scalar(out=g_mz[:, 0:n], in0=g_z[:, 0:n], scalar1=0.0, scalar2=None, op0=ALU.min), r=[gzb], w=[gmzb])
                        op("act", lambda a: a.activation(out=g_z[:, 0:n], in_=g_z[:, 0:n], func=AF.Abs), r=[gzb], w=[gzb])
                        op("act", lambda a: a.activation(out=g_z[:, 0:n], in_=g_z[:, 0:n], func=AF.Exp, scale=-1.0), r=[gzb], w=[gzb])
                        op("act", lambda a: a.activation(out=g_z[:, 0:n], in_=g_z[:, 0:n], func=AF.Ln, bias=1.0), r=[gzb], w=[gzb])
                        op("dve", lambda v: v.tensor_tensor(out=g_mz[:, 0:n], in0=g_mz[:, 0:n], in1=g_z[:, 0:n], op=ALU.subtract), r=[gzb, gmzb], w=[gmzb])

                    def qk_proj(c0, n, hbufs, dst, dstb):
                        for j in range(8):
                            def mm(pe, j=j):
                                for k in range(KC):
                                    ins = pe.matmul(ps[:, 7, 0:n], lhsT=wqk[:, k, j * 128:(j + 1) * 128], rhs=hy[:, k, c0:c0 + n], start=(k == 0), stop=(k == KC - 1))
                                return ins
                            op("pe", mm, r=[wqkb] + hbufs, w=[PB[7]])
                            if j < 4:
                                op("act", lambda a, j=j: a.activation(out=dst[:, j, 0:n], in_=ps[:, 7, 0:n], func=AF.Copy, scale=DQK ** -0.5), r=[PB[7]], w=[dstb])
                            else:
                                op("dve", lambda v, j=j: v.tensor_copy(out=dst[:, j, 0:n], in_=ps[:, 7, 0:n]), r=[PB[7]], w=[dstb])

                    hs_b = [hyB[NBLK]]
                    qk_proj(T, NS, hs_b, qkTs, qkTsb)
                    for nb in range(4):
                        for mi in range(4):
                            def mm(pe, nb=nb, mi=mi):
                                for k in range(KC):
                                    ins = pe.matmul(ps[:, 7, 0:NS], lhsT=wvo_s[nb][:, k, mi * 128:(mi + 1) * 128], rhs=hy[:, k, T:TT], start=(k == 0), stop=(k == KC - 1))
                                return ins
                            op("pe", mm, r=[ringB[nb]] + hs_b, w=[PB[7]])
                            op("dve", lambda v, nb=nb, mi=mi: v.tensor_copy(out=voTs[:, nb * 4 + mi, :], in_=ps[:, 7, 0:NS]), r=[PB[7]], w=[voTsb])
                            stage(3)
                    gate_pre(T, NS, hs_b)
                    igs, lfm = sm1[:, 32:48], g_mz[:, 0:NS]
                    op("dve", lambda v: v.tensor_scalar(out=igs, in0=g_th[:, 0:NS], scalar1=CAP, scalar2=None, op0=ALU.mult), r=[gthb], w=[sm1b])
                    op("dve", lambda v: v.tensor_tensor(out=lfm, in0=lfm, in1=sm1[:, 16:32], op=ALU.add), r=[gmzb, sm1b], w=[gmzb])
                    op("dve", lambda v: v.tensor_tensor(out=gs_rows[:, 0, :], in0=lfm, in1=igs, op=ALU.max), r=[gmzb, sm1b], w=[gsrb])
                    op("dve", lambda v: v.tensor_tensor(out=gs_rows[:, 1, :], in0=lfm, in1=gs_rows[:, 0, :], op=ALU.subtract), r=[gmzb, gsrb], w=[gsrb])
                    op("dve", lambda v: v.tensor_tensor(out=gs_rows[:, 2, :], in0=igs, in1=gs_rows[:, 0, :], op=ALU.subtract), r=[sm1b, gsrb], w=[gsrb])
                    op("act", lambda a: a.activation(out=gs_rows[:, 1:3, :], in_=gs_rows[:, 1:3, :], func=AF.Exp), r=[gsrb], w=[gsrb])
                    op("act", lambda a: a.activation(out=gs_rows[:, 3, :], in_=gs_rows[:, 0, :], func=AF.Exp, scale=-1.0), r=[gsrb], w=[gsrb])

                    def tr_gs(pe):
                        for q in range(4):
                            ins = pe.transpose(ps[0:NS, 7, q * 8:(q + 1) * 8], gs_rows[:, q, :], ident[0:8, 0:8])
                        return ins
                    op("pe", tr_gs, r=[gsrb, constb], w=[PB[7]])
                    op("dve", lambda v: v.tensor_copy(out=gs_tok[:].rearrange("p a b -> p (a b)"), in_=ps[0:NS, 7, 0:32]), r=[PB[7]], w=[gstb])
                    stage(4)

                    psb7k = ps[:, 7, 0:256].bitcast(BF16)

                    def stage1(b):
                        c0 = (b % 4) * 128
                        hyb = hyB[b]
                        so, sob = so2[b % 2], so2b[b % 2]
                        for nb in range(4):
                            def mm(pe, nb=nb):
                                for k in range(KC):
                                    ins = pe.matmul(ps[:, nb, :], lhsT=hy[:, k, b * 128:(b + 1) * 128], rhs=wvo_s[nb][:, k, :], start=(k == 0), stop=(k == KC - 1))
                                return ins
                            op("pe", mm, r=[hyb, ringB[nb]], w=[PB[nb]])

                        def tr_k(pe):
                            for j in range(4):
                                ins = pe.transpose(psb7k[:, j * 128:(j + 1) * 128], qkT[:, 4 + j, c0:c0 + 128], identb[:])
                            return ins
                        op("pe", tr_k, r=[qkTb, constb], w=[PB[7]])
                        op("act", lambda a: a.copy(out=ktok[:], in_=psb7k), r=[PB[7]], w=[ktokb])
                        op("dve", lambda v: v.tensor_tensor(out=vaug[:, :, 0:DV], in0=ps[:, 0:2, :].rearrange("p b (h e) -> p (b h) e", e=DV), in1=ef_tok[:, b, 0, :, None].broadcast_to([128, H, DV]), op=ALU.mult), r=[PB[0], PB[1], efb], w=[vaugb])
                        op("dve", lambda v: v.tensor_copy(out=vaug[:, :, DV:DV + 1], in_=ef_tok[:, b, 0, :, None]), r=[efb], w=[vaugb])
                        op("act", lambda a: a.activation(out=so[:], in_=ps[:, 2:4, :].rearrange("p b (h e) -> p (b h) e", e=DV), func=AF.Sigmoid), r=[PB[2], PB[3]], w=[sob])

                        def mm_s(pe):
                            for hd in range(H):
                                po = (hd % 2) * 64
                                ins = pe.matmul(ps[:, hd % 2, (hd // 2) * 128:(hd // 2 + 1) * 128], lhsT=qkT[po:po + 64, 4 + hd // 2, c0:c0 + 128], rhs=qkT[po:po + 64, hd // 2, c0:c0 + 128], start=True, stop=True)
                            return ins
                        op("pe", mm_s, r=[qkTb], w=[PB[0], PB[1]])
                        op("dve", lambda v: v.tensor_tensor(out=Sm[:], in0=ps[:, 0:2, :].rearrange("p b (h t) -> p (b h) t", t=128), in1=cmask[:, None, :].broadcast_to([128, H, 128]), op=ALU.mult), r=[PB[0], PB[1], constb], w=[Smb])
                        if b == 0:
                            stage(6)

                    def stage2(b):
                        c0 = (b % 4) * 128
                        so, sob = so2[b % 2], so2b[b % 2]

                        def mm_n(pe):
                            for hd in range(H):
                                po = (hd % 2) * 64
                                o_ap = ps[:, 4 + hd // 3, (hd % 3) * AUG:(hd % 3) * AUG + NA]
                                pe.matmul(o_ap, lhsT=qkT[po:po + 64, hd // 2, c0:c0 + 128], rhs=Cbf[po:po + 64, hd // 2, 0:NA], start=True, stop=False)
                                ins = pe.matmul(o_ap, lhsT=Sm[:, (hd % 2) * 4 + hd // 2, :], rhs=vaug[:, hd, 0:NA], start=False, stop=True)
                            return ins
                        op("pe", mm_n, r=[qkTb, Cbfb, Smb, vaugb], w=[PB[4], PB[5], PB[6]])

                        def mm_d(pe):
                            for hd in range(H):
                                po = (hd % 2) * 64
                                j = hd // 2
                                o_ap = ps[po:po + 64, 2, j * AUG:j * AUG + NA] if j < 3 else ps[po:po + 64, 3, 0:NA]
                                ins = pe.matmul(o_ap, lhsT=ktok[:, hd * 64:(hd + 1) * 64], rhs=vaug[:, hd, 0:NA], start=True, stop=True)
                            return ins
                        op("pe", mm_d, r=[ktokb, vaugb], w=[PB[2], PB[3]])
                        op("dve", lambda v: v.tensor_tensor(out=Cst[:, 0:3, 0:NA], in0=Cst[:, 0:3, 0:NA], in1=ps[:, 2, 0:3 * AUG].rearrange("p (j e) -> p j e", j=3)[:, :, 0:NA], op=ALU.add), r=[PB[2], Cstb], w=[Cstb])
                        op("dve", lambda v: v.tensor_tensor(out=Cst[:, 3, 0:NA], in0=Cst[:, 3, 0:NA], in1=ps[:, 3, 0:NA], op=ALU.add), r=[PB[3], Cstb], w=[Cstb])
                        for half in range(2):
                            po = half * 64
                            op("dve", lambda v, po=po, half=half: v.tensor_tensor(out=Cst[po:po + 64, :, 0:NA], in0=Cst[po:po + 64, :, 0:NA], in1=decb[po:po + 64, b, half::2][:, :, None].broadcast_to([64, 4, NA]), op=ALU.mult), r=[Cstb, decbb], w=[Cstb])
                        op("act", lambda a: a.copy(out=Cbf[:], in_=Cst[:]), r=[Cstb], w=[Cbfb])
                        if b == 0:
                            stage(7)
                        groups = []
                        for gi, (h0, g) in enumerate(((0, 3), (3, 3), (6, 2))):
                            na = ps[:, 4 + gi, 0:g * AUG].rearrange("p (h e) -> p h e", e=AUG)
                            groups.append((na[:, :, 0:DV], h0, g, [PB[4 + gi]]))
                            op("act", lambda a, na=na, h0=h0, g=g: a.activation(out=sm8[:, 0, h0:h0 + g], in_=na[:, :, DV], func=AF.Abs), r=[PB[4 + gi]], w=[sm8b])
                        ml_fin_a(128, groups, sm8[:, 0, :], sm8b, ef_tok[:, b, 1, :], efb, so[:], sob, yt, ytb, ybf, ybfb, sm8, sm8b)

                    def stage3(b):
                        ml_fin_b(128, ybf, ybfb, b * 128, [hyB[b]])
                        if b == 0:
                            tap("ml_y0", hy[:, 0, 0:128], [hyB[0]])
                            stage(8)

                    for tt in range(4):
                        c0t = tt * 512
                        hbt = hyB[4 * tt:4 * tt + 4]
                        qk_proj(c0t, 512, hbt, qkT, qkTb)
                        gate_pre(c0t, 512, hbt)
                        op("dve", lambda v: v.tensor_tensor_scan(out=g_z[:], data0=ones8[:], data1=g_mz[:], initial=sm1[:, 0:1], op0=ALU.mult, op1=ALU.add), r=[gmzb, o8b, sm1b], w=[gzb])
                        op("dve", lambda v: v.scalar_tensor_tensor(out=g_th[:], in0=g_th[:], scalar=CAP, in1=g_z[:], op0=ALU.mult, op1=ALU.subtract), r=[gthb, gzb], w=[gthb])
                        op("dve", lambda v: v.tensor_copy(out=Mp[:, 127:128], in_=sm1[:, 1:2]), r=[sm1b], w=[Mpb])
                        op("dve", lambda v: v.tensor_tensor_scan(out=Mp[:, 128:640], data0=ones8[:], data1=g_th[:], initial=sm1[:, 1:2], op0=ALU.mult, op1=ALU.max), r=[gthb, o8b, sm1b, Mpb], w=[Mpb])
                        op("dve", lambda v: v.tensor_copy(out=sm1[:, 0:1], in_=g_z[:, 511:512]), r=[gzb], w=[sm1b])
                        op("dve", lambda v: v.tensor_copy(out=sm1[:, 1:2], in_=Mp[:, 639:640]), r=[Mpb], w=[sm1b])
                        if tt == 3:
                            op("dve", lambda v: v.tensor_tensor(out=sm1[:, 48:49], in0=g_z[:, 511:512], in1=Mp[:, 639:640], op=ALU.add), r=[gzb, Mpb], w=[sm1b])
                            op("dve", lambda v: v.memset(decr[:], 0.0), w=[decrb])
                            op("dve", lambda v: v.tensor_copy(out=decr[:, 0, 0:1], in_=sm1[:, 48:49]), r=[sm1b, decrb], w=[decrb])
                            op("pe", lambda pe: pe.transpose(ps[0:32, 7, 128:136], decr[:].rearrange("p a b -> p (a b)"), ident[0:8, 0:8]), r=[decrb, constb], w=[PB[7]])
                            op("dve", lambda v: v.tensor_copy(out=sm8[0:32, 5, :], in_=ps[0:32, 7, 128:136]), r=[PB[7]], w=[sm8b])
                            dma("sp", m_p.rearrange("h o -> o h"), sm8[0:1, 5, :], r=[sm8b])
                        Mp5 = Mp[:].rearrange("p (b j) -> p b j", j=128)
                        mref = Mp5[:, 0:4, 127:128]
                        op("dve", lambda v: v.tensor_tensor(out=sm1[:, 4:8], in0=Mp5[:, 0:4, 127], in1=Mp5[:, 1:5, 127], op=ALU.subtract), r=[Mpb], w=[sm1b])
                        op("act", lambda a: a.activation(out=sm1[:, 4:8], in_=sm1[:, 4:8], func=AF.Exp), r=[sm1b], w=[sm1b])
                        a4 = g_th[:].rearrange("p (b j) -> p b j", j=128)
                        b4 = g_z[:].rearrange("p (b j) -> p b j", j=128)
                        op("dve", lambda v: v.tensor_tensor(out=a4, in0=a4, in1=mref.broadcast_to([8, 4, 128]), op=ALU.subtract), r=[gthb, Mpb], w=[gthb])
                        op("act", lambda a: a.activation(out=g_th[:], in_=g_th[:], func=AF.Exp), r=[gthb], w=[gthb])
                        op("dve", lambda v: v.scalar_tensor_tensor(out=b4, in0=b4, scalar=-1.0, in1=mref.broadcast_to([8, 4, 128]), op0=ALU.mult, op1=ALU.subtract), r=[gzb, Mpb], w=[gzb])
                        op("act", lambda a: a.activation(out=g_z[:], in_=g_z[:], func=AF.Exp), r=[gzb], w=[gzb])

                        def tr_ef(pe):
                            for bl in range(4):
                                pe.transpose(ps[:, 7, (bl * 2) * 8:(bl * 2 + 1) * 8], g_th[:, bl * 128:(bl + 1) * 128], ident[0:8, 0:8])
                                ins = pe.transpose(ps[:, 7, (bl * 2 + 1) * 8:(bl * 2 + 2) * 8], g_z[:, bl * 128:(bl + 1) * 128], ident[0:8, 0:8])
                            return ins
                        op("pe", tr_ef, r=[gthb, gzb, constb], w=[PB[7]])
                        op("dve", lambda v, tt=tt: v.tensor_copy(out=ef_tok[:, 4 * tt:4 * tt + 4, :, :].rearrange("p a b c -> p (a b c)"), in_=ps[:, 7, 0:64]), r=[PB[7]], w=[efb])
                        op("dve", lambda v: v.tensor_tensor(out=decr[:], in0=sm1[:, 4:8, None].broadcast_to([8, 4, 8]), in1=ident[0:8, None, 0:8].broadcast_to([8, 4, 8]), op=ALU.mult), r=[sm1b, constb], w=[decrb])
                        op("pe", lambda pe: pe.matmul(ps[:, 7, 64:96], lhsT=onesf[:], rhs=decr[:].rearrange("p a b -> p (a b)"), start=True, stop=True), r=[o8b, decrb], w=[PB[7]])
                        op("dve", lambda v, tt=tt: v.tensor_copy(out=decb[:, 4 * tt:4 * tt + 4, :].rearrange("p a b -> p (a b)"), in_=ps[:, 7, 64:96]), r=[PB[7]], w=[decbb])
                        stage(5)

                        for bl in range(4):
                            b = 4 * tt + bl
                            stage1(b)
                            if b > 0:
                                stage3(b - 1)
                            stage2(b)
                    stage3(NBLK - 1)
                    for half in range(2):
                        po = half * 64
                        dma("sp", C_p.rearrange("(j h) d e -> h d j e", h=2)[half], Cst[po:po + 64, :, 0:DV], r=[Cstb])
                    op("dve", lambda v: v.memset(yt[:, 0, 0:32], 0.0), w=[ytb])
                    op("dve", lambda v: v.tensor_copy(out=yt[:, 0, 0:4], in_=Cst[:, :, DV]), r=[Cstb], w=[ytb])
                    op("pe", lambda pe: pe.transpose(ps[0:32, 7, 0:128], yt[:, 0, 0:32], ident[:]), r=[ytb, constb], w=[PB[7]])
                    op("dve", lambda v: v.tensor_copy(out=so2[0][0:32, 0, :], in_=ps[0:32, 7, 0:128]), r=[PB[7]], w=[so2b[0]])
                    dma("sp", n_p.rearrange("(j h) d -> j (h d)", h=2), so2[0][0:4, 0, :], r=[so2b[0]])
                    all_engines_finish([wqkb, wifb, qkTb, ktokb, vaugb, ybfb, Smb, ytb, Cstb, Cbfb, gthb, gzb, gmzb, Mpb, o8b, sm1b, decrb, efb, decbb, sm8b] + ringB + so2b)
                    stage(9)
                with ExitStack() as s2:
                    v_tok = sb("s_vtok", [NS, H, DV], stack=s2)
                    so_tok = sb("s_sotok", [NS, H, DV], stack=s2)
                    qk_tok = sb("s_qktok", [NS, 2, H, DQK], stack=s2)
                    kbf = sb("s_kbf", [NS, H * DQK], BF16, stack=s2)
                    numC = sb("s_numC", [NS, H, DV], stack=s2)
                    vw = sb("s_vw", [NS, H, DV], stack=s2)
                    yts = sb("s_yt", [NS, H, DV], stack=s2)
                    ybfs = sb("s_ybf", [NS, H, DV], BF16, stack=s2)
                    ybfsb = Buf()
                    n0t = sb("s_n0t", [NS, H, DQK], stack=s2)
                    prod = sb("s_prod", [NS, H, DQK], stack=s2)
                    sm8s = sb("s_sm8", [NS, 10, H], stack=s2)
                    C0j = [sb(f"s_C0j{i}", [128, NS, DV], stack=s2) for i in range(2)]
                    wstb = sb("s_wstb", [128, 4, NS], stack=s2)
                    Psel = sb("s_Psel", [8, 128], stack=s2)
                    Dsel = sb("s_Dsel", [8, 8], stack=s2)
                    Wd = sb("s_Wd", [8, 4, NS], stack=s2)
                    vtb, sotb, qktb, kbfb, numCb, vwb, ytsb, n0b, prodb, sm8sb, wstbb, selb, Wdb = [Buf() for _ in range(13)]
                    C0jb = [Buf(), Buf()]
                    C0bf = [ring[:, i, 0:NS * DV].rearrange("p (i e) -> p i e", e=DV) for i in range(2)]
                    Vbd = ring[0:NS, 2, 0:NS * DV].rearrange("p (i e) -> p i e", e=DV)
                    tmpd = ring[0:NS, 3, :].bitcast(F32).rearrange("p (i e) -> p i e", e=DV)
                    mt_t, wst_t, wi_t, fls_t = [gs_tok[:, q, :] for q in range(4)]
                    dma("sp", n0t[:], st_n, w=[n0b])
                    op("dve", lambda v: v.tensor_reduce(out=Dsel[:, 4:5], in_=ident[0:8, 0:8:2], axis=AX.X, op=ALU.add), r=[constb], w=[selb])
                    op("dve", lambda v: v.tensor_reduce(out=Dsel[:, 5:6], in_=ident[0:8, 1:8:2], axis=AX.X, op=ALU.add), r=[constb], w=[selb])
                    op("dve", lambda v: v.tensor_copy(out=Psel[:, 0:64], in_=Dsel[:, 4:5].broadcast_to([8, 64])), r=[selb], w=[selb])
                    op("dve", lambda v: v.tensor_copy(out=Psel[:, 64:128], in_=Dsel[:, 5:6].broadcast_to([8, 64])), r=[selb], w=[selb])
                    op("dve", lambda v: v.tensor_tensor(out=Dsel[:, 0:4], in0=ident[0:8, 0:8:2], in1=ident[0:8, 1:8:2], op=ALU.add), r=[constb], w=[selb])
                    op("dve", lambda v: v.tensor_tensor(out=Wd[:], in0=gs_rows[:, 1, None, :].broadcast_to([8, 4, NS]), in1=Dsel[:, 0:4, None].broadcast_to([8, 4, NS]), op=ALU.mult), r=[gsrb, selb], w=[Wdb])
                    op("pe", lambda pe: pe.matmul(ps[:, 7, 0:4 * NS], lhsT=Psel[:], rhs=Wd[:].rearrange("p a b -> p (a b)"), start=True, stop=True), r=[selb, Wdb], w=[PB[7]])
                    op("dve", lambda v: v.tensor_copy(out=wstb[:].rearrange("p a b -> p (a b)"), in_=ps[:, 7, 0:4 * NS]), r=[PB[7]], w=[wstbb])
                    def tr_v(pe, base, bank0):
                        for c in range(8):
                            ins = pe.transpose(ps[0:NS, bank0 + c // 4, (c % 4) * 128:(c % 4 + 1) * 128], voTs[:, base + c, :], ident[:])
                        return ins
                    op("pe", lambda pe: tr_v(pe, 0, 0), r=[voTsb, constb], w=[PB[0], PB[1]])
                    op("dve", lambda v: v.tensor_copy(out=v_tok[:], in_=ps[0:NS, 0:2, :].rearrange("p b (h e) -> p (b h) e", e=DV)), r=[PB[0], PB[1]], w=[vtb])
                    op("pe", lambda pe: tr_v(pe, 8, 2), r=[voTsb, constb], w=[PB[2], PB[3]])
                    op("act", lambda a: a.activation(out=so_tok[:], in_=ps[0:NS, 2:4, :].rearrange("p b (h e) -> p (b h) e", e=DV), func=AF.Sigmoid), r=[PB[2], PB[3]], w=[sotb])
                    psb4 = ps[:, 4, :].bitcast(BF16)

                    def tr_qk(pe):
                        for c in range(8):
                            ins = pe.transpose(psb4[0:NS, c * 128:(c + 1) * 128], qkTs[:, c, :], identb[:])
                        return ins
                    op("pe", tr_qk, r=[qkTsb, constb], w=[PB[4]])
                    op("dve", lambda v: v.tensor_copy(out=qk_tok[:].rearrange("p a h d -> p (a h d)"), in_=psb4[0:NS, :]), r=[PB[4]], w=[qktb])
                    op("act", lambda a: a.copy(out=kbf[:], in_=psb4[0:NS, 512:1024]), r=[PB[4]], w=[kbfb])
                    qkd, qn, s_, den = [sm8s[:, i, :] for i in (6, 7, 8, 9)]
                    op("dve", lambda v: v.tensor_tensor(out=prod[:], in0=qk_tok[:, 0, :, :], in1=qk_tok[:, 1, :, :], op=ALU.mult), r=[qktb], w=[prodb])
                    op("dve", lambda v: v.tensor_reduce(out=qkd, in_=prod[:], axis=AX.X, op=ALU.add), r=[prodb], w=[sm8sb])
                    op("dve", lambda v: v.tensor_tensor(out=prod[:], in0=qk_tok[:, 0, :, :], in1=n0t[:], op=ALU.mult), r=[qktb, n0b, prodb], w=[prodb])
                    op("dve", lambda v: v.tensor_reduce(out=qn, in_=prod[:], axis=AX.X, op=ALU.add), r=[prodb], w=[sm8sb])
                    op("dve", lambda v: v.tensor_tensor(out=s_, in0=qkd, in1=wi_t, op=ALU.mult), r=[sm8sb, gstb], w=[sm8sb])
                    op("dve", lambda v: v.tensor_tensor(out=den, in0=qn, in1=wst_t, op=ALU.mult), r=[sm8sb, gstb], w=[sm8sb])
                    op("dve", lambda v: v.tensor_tensor(out=den, in0=den, in1=s_, op=ALU.add), r=[sm8sb], w=[sm8sb])
                    op("dve", lambda v: v.tensor_tensor(out=vw[:], in0=v_tok[:], in1=wi_t[:, :, None].broadcast_to([NS, H, DV]), op=ALU.mult), r=[vtb, gstb], w=[vwb])
                    op("dve", lambda v: v.tensor_tensor(out=prod[:], in0=qk_tok[:, 1, :, :], in1=wi_t[:, :, None].broadcast_to([NS, H, DQK]), op=ALU.mult), r=[qktb, gstb, prodb], w=[prodb])
                    op("dve", lambda v: v.tensor_tensor(out=n0t[:], in0=n0t[:], in1=wst_t[:, :, None].broadcast_to([NS, H, DQK]), op=ALU.mult), r=[n0b, gstb], w=[n0b])
                    op("dve", lambda v: v.tensor_tensor(out=n0t[:], in0=n0t[:], in1=prod[:], op=ALU.add), r=[n0b, prodb], w=[n0b])
                    dma("sp", n_s, n0t[:], r=[n0b])
                    dma("sp", m_s, mt_t, r=[gstb])
                    for j in range(4):
                        Cj, Cjb = C0j[j % 2], C0jb[j % 2]
                        Cb, Cbb = C0bf[j % 2], ringB[j % 2]
                        for half in range(2):
                            po = half * 64
                            dma("sp", Cj[po:po + 64, :, :], st_C[:, 2 * j + half, :, :].rearrange("i d e -> d i e"), w=[Cjb])
                        op("act", lambda a, Cj=Cj, Cb=Cb: a.copy(out=Cb, in_=Cj[:]), r=[Cjb], w=[Cbb])
                        for half in range(2):
                            po = half * 64
                            hd = 2 * j + half

                            def mm_c(pe, po=po, j=j, Cb=Cb):
                                for nb in range(4):
                                    ins = pe.matmul(ps[0:NS, nb, :], lhsT=qkTs[po:po + 64, j, :], rhs=Cb[po:po + 64, nb * 4:(nb + 1) * 4, :].rearrange("p i e -> p (i e)"), start=True, stop=True)
                                return ins
                            op("pe", mm_c, r=[qkTsb, Cbb], w=[PB[0], PB[1], PB[2], PB[3]])
                            op("dve", lambda v: v.tensor_tensor(out=tmpd, in0=ps[0:NS, 0:4, :].rearrange("p b (i e) -> p (b i) e", e=DV), in1=ident[0:NS, 0:NS, None].broadcast_to([NS, NS, DV]), op=ALU.mult), r=[PB[0], PB[1], PB[2], PB[3], constb], w=[ringB[3]])
                            op("dve", lambda v, hd=hd: v.tensor_reduce(out=numC[:, hd, :], in_=tmpd.rearrange("p i e -> p e i"), axis=AX.X, op=ALU.add), r=[ringB[3]], w=[numCb])
                            op("dve", lambda v, hd=hd: v.tensor_tensor(out=Vbd, in0=vw[:, hd, None, :].broadcast_to([NS, NS, DV]), in1=ident[0:NS, 0:NS, None].broadcast_to([NS, NS, DV]), op=ALU.mult), r=[vwb, constb], w=[ringB[2]])

                            def mm_u(pe, po=po, hd=hd):
                                for nb in range(4):
                                    ins = pe.matmul(ps[po:po + 64, 4 + nb, :], lhsT=kbf[:, hd * 64:(hd + 1) * 64], rhs=Vbd[:, nb * 4:(nb + 1) * 4, :].rearrange("p i e -> p (i e)"), start=True, stop=True)
                                return ins
                            op("pe", mm_u, r=[kbfb, ringB[2]], w=[PB[4], PB[5], PB[6], PB[7]])
                        op("dve", lambda v, Cj=Cj, j=j: v.tensor_tensor(out=Cj[:], in0=Cj[:], in1=wstb[:, j, :, None].broadcast_to([128, NS, DV]), op=ALU.mult), r=[Cjb, wstbb], w=[Cjb])
                        op("dve", lambda v, Cj=Cj: v.tensor_tensor(out=Cj[:], in0=Cj[:], in1=ps[:, 4:8, :].rearrange("p b (i e) -> p (b i) e", e=DV), op=ALU.add), r=[Cjb, PB[4], PB[5], PB[6], PB[7]], w=[Cjb])
                        for half in range(2):
                            po = half * 64
                            dma("sp", C_s[:, 2 * j + half, :, :].rearrange("i d e -> d i e"), Cj[po:po + 64, :, :], r=[Cjb])
                    op("dve", lambda v: v.tensor_tensor(out=numC[:], in0=numC[:], in1=wst_t[:, :, None].broadcast_to([NS, H, DV]), op=ALU.mult), r=[numCb, gstb], w=[numCb])
                    op("dve", lambda v: v.tensor_tensor(out=vw[:], in0=v_tok[:], in1=s_[:, :, None].broadcast_to([NS, H, DV]), op=ALU.mult), r=[vtb, sm8sb, vwb], w=[vwb])
                    op("dve", lambda v: v.tensor_tensor(out=numC[:], in0=numC[:], in1=vw[:], op=ALU.add), r=[numCb, vwb], w=[numCb])
                    op("act", lambda a: a.activation(out=den, in_=den, func=AF.Abs), r=[sm8sb], w=[sm8sb])
                    ml_fin_a(NS, [(numC[:], 0, H, [numCb])], den, sm8sb, fls_t, gstb, so_tok[:], sotb, yts, ytsb, ybfs, ybfsb, sm8s, sm8sb)
                    ml_fin_b(NS, ybfs, ybfsb, T, [hyB[NBLK]])
                    tap("ml_ys", hy[:, 0, T:TT], [hyB[NBLK]])
                    all_engines_finish([vtb, sotb, qktb, kbfb, numCb, vwb, ytsb, n0b, prodb, sm8sb, wstbb, selb, Wdb, voTsb, qkTsb, gsrb, gstb, ybfsb] + C0jb + ringB + hyB)
                with ExitStack() as s2:
                    S = alloc_scratch(s2)
                    wout = sb("m_wout", [128, KC, D], BF16, stack=s2)
                    woutb = Buf()
                    out_ts = [sb(f"m_outt{i}", [128, KC, 512], stack=s2) for i in range(2)]
                    out_tb = [Buf(), Buf()]
                    zbt = [Buf() for _ in range(5)]
                    dma("pool", wout[:], ml_w_out[0].rearrange("(k p) n -> p k n", p=128), w=[woutb])
                    out_proj_update(sub, wout, woutb, lambda ti: hy[:, :, tiles[ti][0]:tiles[ti][0] + tiles[ti][1]], lambda ti: zbt[ti], [0, 1, 2, 3, 4], out_ts, out_tb, S)
                    all_engines_finish(scratch_bufs(S) + [woutb] + out_tb + zbt)

        STOP = taps.get("_stop", None)
        PROG = taps.get("_prog", "full")
        if PROG == "full":
            g0 = adaln_blocks(0, 7)
            steps(g0, 6)
            adaln_abg(0, [0])
            ffn(0, 0, 0, bg=g0, bg_n=1)
            steps(g0, 18)
            adaln_abg(0, [1, 2])
            mlstm_mixer(1)
            g1 = adaln_blocks(1, 7)
            ffn(0, 1, 2, bg=g1, bg_n=2)
            steps(g1, 18)
            adaln_abg(1, [0, 1, 2])
            ffn(1, 0, 0)
            conv_mixer(1)
            ffn(1, 1, 2)
        else:
            for step in PROG.split():
                if step[0] == "a":
                    g_ = adaln_blocks(int(step[1]), 7)
                    steps(g_, 18)
                    adaln_abg(int(step[1]), [0, 1, 2])
                elif step[0] == "f":
                    ffn(int(step[1]), int(step[2]), 0 if step[2] == "0" else 2)
                elif step[0] == "c":
                    conv_mixer(1)
                elif step[0] == "m":
                    mlstm_mixer(1)
        if K.stopped:
            K.stopped = False
            for e_ in ("pe", "act", "dve", "pool", "sp"):
                K.finish(e_, list(ALL_BUFS))

        with ExitStack() as s9:
            ys = [sb(f"ys{i}", [128, D], stack=s9) for i in range(2)]
            ysb = [Buf() for _ in range(2)]
            for blk in range(NBLK + 1):
                i = blk % 2
                n = 128 if blk < NBLK else NS
                xb = xB[blk // 4] if blk < NBLK else xB[4]
                pb = 2 * (blk % 4)

                def tr_y(pe, blk=blk, n=n, pb=pb):
                    for k in range(KC):
                        ins = pe.transpose(ps[0:n, pb + k // 4, (k % 4) * 128:(k % 4 + 1) * 128], xT[:, k, blk * 128:blk * 128 + n], ident[:])
                    return ins
                op("pe", tr_y, r=[xb, constb], w=[PB[pb], PB[pb + 1]])
                if blk % 2 == 0:
                    op("dve", lambda v, i=i, n=n, pb=pb: v.tensor_copy(out=ys[i][0:n, :], in_=ps[0:n, pb:pb + 2, :].rearrange("p b t -> p (b t)")), r=[PB[pb], PB[pb + 1]], w=[ysb[i]])
                else:
                    op("act", lambda a, i=i, n=n, pb=pb: a.copy(out=ys[i][0:n, :], in_=ps[0:n, pb:pb + 2, :].rearrange("p b t -> p (b t)")), r=[PB[pb], PB[pb + 1]], w=[ysb[i]])
                dst = y_p[blk * 128:(blk + 1) * 128, :] if blk < NBLK else y_s
                dma("sp", dst, ys[i][0:n, :], r=[ysb[i]])
            all_engines_finish(ysb)
        for q in ("sp", "pool"):
            for i, s in enumerate(K.dma_sems[q]):
                cntv = (K.dma_i[q] - i + len(K.dma_sems[q]) - 1) // len(K.dma_sems[q])
                if cntv > 0:
                    nc.sync.wait_ge(s, 16 * cntv)
    return nc


_PROG = {}


def _get_prog(taps=None):
    key = tuple(sorted((k, tuple(v) if isinstance(v, (list, tuple)) else v) for k, v in (taps or {}).items()))
    if key not in _PROG:
        _PROG[key] = build_program(taps)
    return _PROG[key]


def make_in_maps(inp):
    f = lambda a: np.ascontiguousarray(np.asarray(a, dtype=np.float32))
    shared = {k: f(inp[k]) for k in ("w_ada", "b_ada", "g_pre", "g_post", "ffn_wg", "ffn_wu", "ffn_wd", "ml_w_in", "ml_b_i",
                                      "ml_b_f", "ml_g_head", "ml_w_out", "cv_w_in", "cv_conv_w", "cv_w_out")}
    maps = []
    for c in range(NCORES):
        sl = slice(NS * c, NS * (c + 1))
        m = dict(shared)
        m["x_p"] = f(inp["x_prompt"][c])
        m["x_s"] = f(inp["x_sample"][sl, 0, :])
        m["c_all"] = f(np.concatenate([np.asarray(inp["c_sample"])[sl], np.asarray(inp["c_prompt"])[c:c + 1]], axis=0))
        m["st_C"] = f(inp["state_mlstm_C"][0, sl])
        m["st_n"] = f(inp["state_mlstm_n"][0, sl])
        m["st_m"] = f(inp["state_mlstm_m"][0, sl])
        m["st_cv"] = f(inp["state_conv"][0, sl])
        maps.append(m)
    return maps


def run(inp, taps=None):
    nc = _get_prog(taps)
    res = run_bass_kernel_spmd(nc, make_in_maps(inp), core_ids=list(range(NCORES)))
    return res.results


def kernel(**inp):
    R = run(inp)
    cat = lambda k: np.concatenate([r[k] for r in R], axis=0)
    y_prompt = np.stack([r["y_p"] for r in R], axis=0)
    y_sample = cat("y_s")[:, None, :]
    pC = np.stack([r["C_p"] for r in R], axis=0)[None]
    pn = np.stack([r["n_p"] for r in R], axis=0)[None]
    pm = np.stack([r["m_p"][:, 0] for r in R], axis=0)[None]
    pb = np.stack([r["cv_p"] for r in R], axis=0)[None]
    sC = cat("C_s")[None]
    sn = cat("n_s")[None]
    sm = cat("m_s")[None]
    sbuf_ = cat("cv_s")[None]
    return tuple(np.ascontiguousarray(a, dtype=np.float32) for a in (y_prompt, y_sample, pC, pn, pm, pb, sC, sn, sm, sbuf_))
```

```python
import numpy as np
from contextlib import ExitStack
import concourse.bass as bass
import concourse.mybir as mybir
from concourse.bass_utils import run_bass_kernel_spmd

F32 = mybir.dt.float32
BF16 = mybir.dt.bfloat16
AF = mybir.ActivationFunctionType
ALU = mybir.AluOpType
AX = mybir.AxisListType

NCORES = 8
D = 1024
KC = 8
T = 2048
NS = 16
TT = T + NS
DFF = 2816
FC = 22
H = 8
DQK = 64
DV = 128
MLIN = 3088
EPS = 1e-6
CAP = 15.0
NBLK = T // 128
AUG = 160
NA = 132

DEBUG_TAPS = {}


ALL_BUFS = []


class StopProg(Exception):
    pass


class Buf:
    __slots__ = ("W", "R", "name")

    def __init__(self, name=""):
        self.W = {}
        self.R = {}
        self.name = name
        ALL_BUFS.append(self)


def _merge(dst, src):
    for k, (s, v) in src.items():
        if k not in dst or dst[k][1] < v:
            dst[k] = (s, v)


class KB:
    def __init__(self, nc, es, n_dma_sems=8):
        self.nc = nc
        self.engs = {"pe": nc.tensor, "act": nc.scalar, "dve": nc.vector, "pool": nc.gpsimd, "sp": nc.sync}
        self.sem = {k: es.enter_context(nc.semaphore("s_" + k)) for k in self.engs}
        self.cnt = {k: 0 for k in self.engs}
        self.waited = {k: {} for k in self.engs}
        self.dma_sems = {q: [es.enter_context(nc.semaphore(f"d_{q}{i}")) for i in range(n_dma_sems)] for q in ("sp", "pool")}
        self.dma_i = {"sp": 0, "pool": 0}
        self.stopped = False

    def _wait(self, e, deps):
        eng = self.engs[e]
        wt = self.waited[e]
        for key, (s, v) in deps.items():
            if wt.get(key, 0) < v:
                eng.wait_ge(s, v)
                wt[key] = v

    def op(self, e, fn, r=(), w=()):
        if self.stopped:
            return None
        deps = {}
        for b in r:
            _merge(deps, b.W)
        for b in w:
            _merge(deps, b.W)
            _merge(deps, b.R)
        self._wait(e, deps)
        ins = fn(self.engs[e])
        self.cnt[e] += 1
        ins.then_inc(self.sem[e], 1)
        tok = {e: (self.sem[e], self.cnt[e])}
        for b in r:
            _merge(b.R, tok)
        for b in w:
            _merge(b.W, tok)
        return ins

    def dma(self, q, out, in_, r=(), w=(), **kw):
        if self.stopped:
            return None
        deps = {}
        for b in r:
            _merge(deps, b.W)
        for b in w:
            _merge(deps, b.W)
            _merge(deps, b.R)
        i = self.dma_i[q]
        self.dma_i[q] += 1
        sems = self.dma_sems[q]
        n = len(sems)
        s = sems[i % n]
        val = 16 * (i // n + 1)
        key = ("dma", q, i % n)
        if val > 16:
            deps[key] = (s, val - 16) if (key not in deps or deps[key][1] < val - 16) else deps[key]
        self._wait(q, deps)
        ins = self.engs[q].dma_start(out=out, in_=in_, **kw).then_inc(s, 16)
        tok = {key: (s, val)}
        for b in r:
            _merge(b.R, tok)
        for b in w:
            _merge(b.W, tok)
        return ins

    def finish(self, e, bufs):
        if self.stopped:
            return
        deps = {}
        for b in bufs:
            _merge(deps, b.W)
            _merge(deps, b.R)
        self._wait(e, deps)


def build_program(taps=None):
    taps = taps or {}
    del ALL_BUFS[:]
    MSTOP = taps.get("_mstop", 0)

    KREF = [None]

    def stage(k):
        if MSTOP == k:
            KREF[0].stopped = True
    nc = bass.Bass("TRN2", target_bir_lowering=False)

    def din(name, shape):
        return nc.dram_tensor(name, list(shape), F32, kind="ExternalInput").ap()

    def dout(name, shape):
        return nc.dram_tensor(name, list(shape), F32, kind="ExternalOutput").ap()

    x_p = din("x_p", [T, D])
    x_s = din("x_s", [NS, D])
    c_all = din("c_all", [NS + 1, D])
    st_C = din("st_C", [NS, H, DQK, DV])
    st_n = din("st_n", [NS, H, DQK])
    st_m = din("st_m", [NS, H])
    st_cv = din("st_cv", [NS, 2, D])
    w_ada = din("w_ada", [2, D, 9 * D])
    b_ada = din("b_ada", [2, 9 * D])
    g_pre = din("g_pre", [2, 3, D])
    g_post = din("g_post", [2, 3, D])
    ffn_wg = din("ffn_wg", [2, 2, D, DFF])
    ffn_wu = din("ffn_wu", [2, 2, D, DFF])
    ffn_wd = din("ffn_wd", [2, 2, DFF, D])
    ml_w_in = din("ml_w_in", [1, D, MLIN])
    ml_b_i = din("ml_b_i", [1, H])
    ml_b_f = din("ml_b_f", [1, H])
    ml_g_head = din("ml_g_head", [1, H, DV])
    ml_w_out = din("ml_w_out", [1, D, D])
    cv_w_in = din("cv_w_in", [1, D, 3 * D])
    cv_conv_w = din("cv_conv_w", [1, 3, D])
    cv_w_out = din("cv_w_out", [1, D, D])

    y_p = dout("y_p", [T, D])
    y_s = dout("y_s", [NS, D])
    C_p = dout("C_p", [H, DQK, DV])
    n_p = dout("n_p", [H, DQK])
    m_p = dout("m_p", [H, 1])
    cv_p = dout("cv_p", [2, D])
    C_s = dout("C_s", [NS, H, DQK, DV])
    n_s = dout("n_s", [NS, H, DQK])
    m_s = dout("m_s", [NS, H])
    cv_s = dout("cv_s", [NS, 2, D])
    tap_out = {k: dout("tap_" + k, shp) for k, shp in taps.items() if not k.startswith("_")}

    with ExitStack() as es:
        E = es.enter_context
        K = KB(nc, es)
        KREF[0] = K
        op, dma = K.op, K.dma

        uid = [0]

        def sb(name, shape, dt=F32, stack=None):
            uid[0] += 1
            return (stack or es).enter_context(nc.sbuf_tensor(f"{name}_{uid[0]}", list(shape), dt))

        xT = sb("xT", [128, KC, TT])
        xB = [Buf(f"x{i}") for i in range(5)]
        ABG = sb("ABG", [128, 3, 3, KC, NS + 1])
        ABGb = Buf("ABG")
        vA = sb("vA", [128, 120])
        vB = sb("vB", [128, 120])
        vC = sb("vC", [128, 32])
        vAb, vBb, vCb = Buf("vA"), Buf("vB"), Buf("vC")
        ident = sb("ident", [128, 128])
        identb = sb("identb", [128, 128], BF16)
        onesb = sb("onesb", [128, 128], BF16)
        cmask = sb("cmask", [128, 128])
        eps_t = sb("eps_t", [128, 1])
        cT = sb("cT", [128, KC, NS + 1], BF16)
        constb = Buf("const")
        cTb = Buf("cT")
        ps = E(nc.psum_tensor("ps", [128, 8, 512], F32))
        PB = [Buf(f"ps{i}") for i in range(8)]
        RING_N = 4
        ring = sb("ring", [128, RING_N, 4096], BF16)
        ringB = [Buf(f"ring{i}") for i in range(RING_N)]
        ring_i = [0]

        def ring_load(src_ap, shape):
            i = ring_i[0] % RING_N
            ring_i[0] += 1
            n = int(np.prod(shape))
            assert n <= 4096
            v = ring[:, i, 0:n]
            if len(shape) == 2:
                v = v.rearrange("p (a b) -> p a b", a=shape[0])
            elif len(shape) == 3:
                v = v.rearrange("p (a b c) -> p a b c", a=shape[0], b=shape[1])
            dma("pool", v, src_ap, w=[ringB[i]])
            return v, ringB[i]

        def tap(name, ap, bufs):
            if name in tap_out:
                dma("pool", tap_out[name], ap, r=bufs)

        op("pool", lambda g: g.memset(ident[:], 0.0), w=[constb])
        op("pool", lambda g: g.affine_select(out=ident[:], in_=ident[:], pattern=[[-1, 128]], compare_op=ALU.not_equal, fill=1.0, base=0, channel_multiplier=1), r=[constb], w=[constb])
        op("pool", lambda g: g.tensor_copy(out=identb[:], in_=ident[:]), r=[constb], w=[constb])
        op("pool", lambda g: g.memset(onesb[:], 1.0), w=[constb])
        op("pool", lambda g: g.memset(eps_t[:], EPS), w=[constb])
        op("pool", lambda g: g.memset(cmask[:], 1.0), w=[constb])
        op("pool", lambda g: g.affine_select(out=cmask[:], in_=cmask[:], pattern=[[1, 128]], compare_op=ALU.is_ge, fill=0.0, base=0, channel_multiplier=-1), r=[constb], w=[constb])

        with ExitStack() as s0:
            rowsA = sb("rowsA", [120, 128], stack=s0)
            rowsB = sb("rowsB", [120, 128], stack=s0)
            rowsC = sb("rowsC", [32, 128], stack=s0)
            crow = sb("crow", [NS + 1, D], stack=s0)
            xs = [sb(f"xs{i}", [128, D], stack=s0) for i in range(3)]
            rAb, rBb, rCb, crb = Buf(), Buf(), Buf(), Buf()
            xsb = [Buf() for _ in range(3)]
            dma("sp", crow[:], c_all, w=[crb])
            dma("sp", rowsA[0:72, :], b_ada[0].rearrange("(c p) -> c p", p=128), w=[rAb])
            dma("sp", rowsA[72:120, :], g_pre.rearrange("l s (c p) -> (l s c) p", p=128), w=[rAb])
            dma("sp", rowsB[0:72, :], b_ada[1].rearrange("(c p) -> c p", p=128), w=[rBb])
            dma("sp", rowsB[72:120, :], g_post.rearrange("l s (c p) -> (l s c) p", p=128), w=[rBb])
            dma("sp", rowsC[0:24, :], cv_conv_w[0].rearrange("j (c p) -> (j c) p", p=128), w=[rCb])
            dma("sp", rowsC[24:32, :], ml_g_head[0], w=[rCb])
            for (rows, rb, n, dst, db) in ((rowsA, rAb, 120, vA, vAb), (rowsB, rBb, 120, vB, vBb), (rowsC, rCb, 32, vC, vCb)):
                op("pe", lambda pe, rows=rows, n=n: pe.transpose(ps[:, 0, 0:n], rows[0:n, :], ident[0:n, 0:n]), r=[rb, constb], w=[PB[0]])
                op("dve", lambda v, dst=dst, n=n: v.tensor_copy(out=dst[:, 0:n], in_=ps[:, 0, 0:n]), r=[PB[0]], w=[db])
            op("act", lambda a: a.activation(out=crow[:], in_=crow[:], func=AF.Silu), r=[crb], w=[crb])

            def tr_c(pe):
                for k in range(KC):
                    ins = pe.transpose(ps[:, 1, k * 32:k * 32 + NS + 1], crow[:, k * 128:(k + 1) * 128], ident[0:NS + 1, 0:NS + 1])
                return ins
            op("pe", tr_c, r=[crb, constb], w=[PB[1]])
            op("dve", lambda v: v.tensor_copy(out=cT[:], in_=ps[:, 1, 0:KC * 32].rearrange("p (k c) -> p k c", c=32)[:, :, 0:NS + 1]), r=[PB[1]], w=[cTb])

            for blk in range(NBLK + 1):
                i = blk % 3
                n = 128 if blk < NBLK else NS
                src = x_p[blk * 128:(blk + 1) * 128, :] if blk < NBLK else x_s
                dma("sp", xs[i][0:n, :], src, w=[xsb[i]])
                pb = (2 + 2 * (blk % 3))

                def tr_x(pe, i=i, n=n, pb=pb):
                    for k in range(KC):
                        ins = pe.transpose(ps[:, pb + k // 4, (k % 4) * 128:(k % 4) * 128 + n], xs[i][0:n, k * 128:(k + 1) * 128], ident[0:n, 0:n])
                    return ins
                op("pe", tr_x, r=[xsb[i], constb], w=[PB[pb], PB[pb + 1]])
                xb = xB[blk // 4] if blk < NBLK else xB[4]
                src_ps = ps[:, pb:pb + 2, :].rearrange("p b (k t) -> p (b k) t", t=128)[:, :, 0:n]
                eng = "dve" if blk % 2 == 0 else "act"
                if eng == "dve":
                    op("dve", lambda v, blk=blk, n=n, src_ps=src_ps: v.tensor_copy(out=xT[:, :, blk * 128:blk * 128 + n], in_=src_ps), r=[PB[pb], PB[pb + 1]], w=[xb])
                else:
                    op("act", lambda a, blk=blk, n=n, src_ps=src_ps: a.copy(out=xT[:, :, blk * 128:blk * 128 + n], in_=src_ps), r=[PB[pb], PB[pb + 1]], w=[xb])
            for e_ in ("pe", "act", "dve", "pool", "sp"):
                K.finish(e_, [rAb, rBb, rCb, crb] + xsb)

        tiles = [(0, 512, 0), (512, 512, 1), (1024, 512, 2), (1536, 512, 3), (T, NS, 4)]

        def rstd_from_sq(sqv, sqb, n, r_ap, rb, bank):
            def mm(pe):
                for k in range(KC):
                    ins = pe.matmul(ps[:, bank, 0:n], lhsT=onesb[:], rhs=sqv[:, k, :], start=(k == 0), stop=(k == KC - 1))
                return ins
            op("pe", mm, r=(list(sqb) if isinstance(sqb, (list, tuple)) else [sqb]) + [constb], w=[PB[bank]])
            op("act", lambda a: a.activation(out=r_ap, in_=ps[:, bank, 0:n], func=AF.Sqrt, bias=eps_t[:, 0:1], scale=1.0 / D), r=[PB[bank], constb], w=[rb])
            op("dve", lambda v: v.reciprocal(out=r_ap, in_=r_ap), r=[rb], w=[rb])

        def sq_rstd(src, sqv, n, r_ap, rb, rbufs, wbufs, bank=7):
            halves = (Buf(), Buf())
            for hi in range(2):
                k0 = 4 * hi
                op("act", lambda a, k0=k0: a.activation(out=sqv[:, k0:k0 + 4, :], in_=src[:, k0:k0 + 4, :], func=AF.Square), r=rbufs, w=list(wbufs) + [halves[hi]])

                def mm(pe, k0=k0):
                    for k in range(k0, k0 + 4):
                        ins = pe.matmul(ps[:, bank, 0:n], lhsT=onesb[:], rhs=sqv[:, k, :], start=(k == 0), stop=(k == KC - 1))
                    return ins
                op("pe", mm, r=[halves[hi], constb], w=[PB[bank]])
            op("act", lambda a: a.activation(out=r_ap, in_=ps[:, bank, 0:n], func=AF.Sqrt, bias=eps_t[:, 0:1], scale=1.0 / D), r=[PB[bank], constb], w=[rb])
            op("dve", lambda v: v.reciprocal(out=r_ap, in_=r_ap), r=[rb], w=[rb])

        def pre_norm(sub, h_ap_fn, hb_fn, tl, S):
            for ti in tl:
                c0, n, xi = tiles[ti]
                j = ti % 2
                r_ap, rb = S["r"][j][:, 0:n], S["rb"][j]
                hv = h_ap_fn(ti)
                hw = hb_fn(ti)
                if ti < 4:
                    sq_rstd(xT[:, :, c0:c0 + n], hv, n, r_ap, rb, [xB[xi]], hw)
                else:
                    op("act", lambda a, c0=c0, n=n, hv=hv: a.activation(out=hv, in_=xT[:, :, c0:c0 + n], func=AF.Square), r=[xB[xi]], w=hw)
                    rstd_from_sq(hv, hw, n, r_ap, rb, 7)
                if ti < 4:
                    for k in range(KC):
                        tm, tmb = S["tmp"][k % 2], S["tmpb"][k % 2]
                        op("dve", lambda v, k=k, c0=c0, n=n, tm=tm, r_ap=r_ap: v.tensor_tensor(out=tm[:, 0:n], in0=xT[:, k, c0:c0 + n], in1=r_ap, op=ALU.mult), r=[xB[xi], rb], w=[tmb])
                        op("act", lambda a, k=k, n=n, tm=tm, hv=hv: a.activation(out=hv[:, k, :], in_=tm[:, 0:n], func=AF.Identity, bias=ABG[:, sub, 1, k, NS:NS + 1], scale=ABG[:, sub, 0, k, NS:NS + 1]), r=[tmb, ABGb], w=hw)
                else:
                    tm, tmb = S["tmp"][0], S["tmpb"][0]
                    tv = tm[:, 0:KC * NS].rearrange("p (k c) -> p k c", c=NS)
                    op("dve", lambda v: v.tensor_tensor(out=tv, in0=xT[:, :, T:TT], in1=r_ap[:, None, :].broadcast_to([128, KC, NS]), op=ALU.mult), r=[xB[4], rb], w=[tmb])
                    op("dve", lambda v: v.tensor_tensor(out=tv, in0=tv, in1=ABG[:, sub, 0, :, 0:NS], op=ALU.mult), r=[tmb, ABGb], w=[tmb])
                    op("dve", lambda v: v.tensor_tensor(out=hv, in0=tv, in1=ABG[:, sub, 1, :, 0:NS], op=ALU.add), r=[tmb, ABGb], w=hw)

        def post_norm_update(sub, out_ap, outb, sq_ap, sqb, ti, S):
            c0, n, xi = tiles[ti]
            j = ti % 2
            r_ap, rb = S["r"][j][:, 0:n], S["rb"][j]
            if ti < 4:
                sq_rstd(out_ap, sq_ap, n, r_ap, rb, [outb], [sqb])
            else:
                op("act", lambda a: a.activation(out=sq_ap, in_=out_ap, func=AF.Square), r=[outb], w=[sqb])
                rstd_from_sq(sq_ap, sqb, n, r_ap, rb, 7)
            if ti < 4:
                for k in range(KC):
                    tm, tmb = S["tmp"][k % 2], S["tmpb"][k % 2]
                    op("dve", lambda v, k=k, n=n, tm=tm: v.tensor_tensor(out=tm[:, 0:n], in0=out_ap[:, k, :], in1=r_ap, op=ALU.mult), r=[outb, rb], w=[tmb])
                    op("dve", lambda v, k=k, c0=c0, n=n, tm=tm: v.scalar_tensor_tensor(out=xT[:, k, c0:c0 + n], in0=tm[:, 0:n], scalar=ABG[:, sub, 2, k, NS:NS + 1], in1=xT[:, k, c0:c0 + n], op0=ALU.mult, op1=ALU.add), r=[tmb, ABGb, xB[xi]], w=[xB[xi]])
            else:
                tm, tmb = S["tmp"][0], S["tmpb"][0]
                tv = tm[:, 0:KC * NS].rearrange("p (k c) -> p k c", c=NS)
                op("dve", lambda v: v.tensor_tensor(out=tv, in0=out_ap, in1=r_ap[:, None, :].broadcast_to([128, KC, NS]), op=ALU.mult), r=[outb, rb], w=[tmb])
                op("dve", lambda v: v.tensor_tensor(out=tv, in0=tv, in1=ABG[:, sub, 2, :, 0:NS], op=ALU.mult), r=[tmb, ABGb], w=[tmb])
                op("dve", lambda v: v.tensor_tensor(out=xT[:, :, T:TT], in0=xT[:, :, T:TT], in1=tv, op=ALU.add), r=[tmb, xB[4]], w=[xB[4]])

        def alloc_scratch(stack):
            S = {}
            S["r"] = [sb(f"r{i}", [128, 512], stack=stack) for i in range(2)]
            S["rb"] = [Buf() for _ in range(2)]
            S["tmp"] = [sb(f"tmp{i}", [128, 512], stack=stack) for i in range(2)]
            S["tmpb"] = [Buf() for _ in range(2)]
            return S

        def scratch_bufs(S):
            return S["rb"] + S["tmpb"]

        def all_engines_finish(bufs):
            for e in ("pe", "act", "dve", "pool", "sp"):
                K.finish(e, bufs)

        mod = sb("mod", [128, 72, NS + 1])
        modb = Buf("mod")

        def adaln_blocks(l, bank):
            vb_cols = (vA, vAb) if l == 0 else (vB, vBb)
            for blk in range(18):
                wv, wb = ring_load(w_ada[l, :, blk * 512:(blk + 1) * 512].rearrange("(k p) n -> p k n", p=128), [KC, 512])

                def mm(pe, wv=wv):
                    for m in range(4):
                        for k in range(KC):
                            ins = pe.matmul(ps[:, bank, m * 32:m * 32 + NS + 1], lhsT=wv[:, k, m * 128:(m + 1) * 128], rhs=cT[:, k, :], start=(k == 0), stop=(k == KC - 1))
                    return ins
                op("pe", mm, r=[wb, cTb], w=[PB[bank]])
                op("dve", lambda v, blk=blk: v.tensor_tensor(
                    out=mod[:, blk * 4:blk * 4 + 4, :],
                    in0=ps[:, bank, 0:128].rearrange("p (m c) -> p m c", c=32)[:, :, 0:NS + 1],
                    in1=vb_cols[0][:, blk * 4:blk * 4 + 4, None].broadcast_to([128, 4, NS + 1]), op=ALU.add),
                    r=[PB[bank], vb_cols[1]], w=[modb])
                yield

        def adaln_abg(l, subs):
            for s in subs:
                gpre = vA[:, 72 + (l * 3 + s) * 8:72 + (l * 3 + s) * 8 + 8]
                gpost = vB[:, 72 + (l * 3 + s) * 8:72 + (l * 3 + s) * 8 + 8]
                sh = mod[:, (3 * s) * 8:(3 * s) * 8 + 8, :]
                sc = mod[:, (3 * s + 1) * 8:(3 * s + 1) * 8 + 8, :]
                gt = mod[:, (3 * s + 2) * 8:(3 * s + 2) * 8 + 8, :]
                gfac = 1.0 if s == 1 else 0.5
                op("dve", lambda v, s=s, sc=sc, gpre=gpre: v.scalar_tensor_tensor(out=ABG[:, s, 0, :, :], in0=sc, scalar=1.0, in1=gpre[:, :, None].broadcast_to([128, KC, NS + 1]), op0=ALU.add, op1=ALU.mult), r=[modb, vAb], w=[ABGb])
                op("dve", lambda v, s=s, sh=sh: v.tensor_copy(out=ABG[:, s, 1, :, :], in_=sh), r=[modb], w=[ABGb])
                op("dve", lambda v, s=s, gt=gt, gfac=gfac: v.tensor_scalar(out=ABG[:, s, 2, :, :], in0=gt, scalar1=1.0, scalar2=gfac, op0=ALU.add, op1=ALU.mult), r=[modb], w=[ABGb])
                op("dve", lambda v, s=s, gpost=gpost: v.tensor_tensor(out=ABG[:, s, 2, :, :], in0=ABG[:, s, 2, :, :], in1=gpost[:, :, None].broadcast_to([128, KC, NS + 1]), op=ALU.mult), r=[ABGb, vBb], w=[ABGb])

        def steps(gen, n):
            for _ in range(n):
                if next(gen, "end") == "end":
                    return

        def ffn(l, fi, sub, bg=None, bg_n=1):
            wg = ffn_wg[l, fi]
            wu = ffn_wu[l, fi]
            wd = ffn_wd[l, fi]
            with ExitStack() as s1:
                S = alloc_scratch(s1)
                GT = 528
                hbufs = [sb(f"hbuf{i}", [128, KC * GT * 2], BF16, stack=s1) for i in range(2)]
                h_vs = [hb_[:, 0:KC * GT].rearrange("p (k t) -> p k t", t=GT) for hb_ in hbufs]
                out_vs = [hb_[:].bitcast(F32).rearrange("p (k t) -> p k t", t=GT) for hb_ in hbufs]
                a_vs = [sb(f"a_v{i}", [128, FC, GT], BF16, stack=s1) for i in range(2)]
                hB = [Buf(), Buf()]
                aB = [Buf(), Buf()]
                sg = [sb(f"sg{i}", [128, 512], stack=s1) for i in range(2)]
                sgb = [Buf() for _ in range(2)]
                groups = [[0], [1], [2], [3, 4]]
                cntA = [0]
                cntB = [0]

                def locs(g):
                    return {ti: (j * 512) for j, ti in enumerate(groups[g])}

                def pre(g):
                    loc = locs(g)
                    h_v = h_vs[g % 2]
                    pre_norm(sub, lambda ti: h_v[:, :, loc[ti]:loc[ti] + tiles[ti][1]], lambda ti: [hB[g % 2]], groups[g], S)
                    if g == 0:
                        tap(f"h_l{l}f{fi}", h_v[:, 0, 0:512], [hB[0]])

                def phaseA(g):
                    loc = locs(g)
                    h_v, a_v = h_vs[g % 2], a_vs[g % 2]
                    for mb in range(6):
                        nm = 4 if mb < 5 else 2
                        wgv, wgb = ring_load(wg[:, mb * 512:mb * 512 + nm * 128].rearrange("(k p) n -> p k n", p=128), [KC, nm * 128])
                        wuv, wub = ring_load(wu[:, mb * 512:mb * 512 + nm * 128].rearrange("(k p) n -> p k n", p=128), [KC, nm * 128])
                        for mi in range(nm):
                            m = mb * 4 + mi
                            for ti in groups[g]:
                                n = tiles[ti][1]
                                lc = loc[ti]
                                bg_, bu_ = (0, 1) if cntA[0] % 2 == 0 else (2, 3)
                                cntA[0] += 1

                                def mm(pe, wv=wgv, bank=bg_, mi=mi, lc=lc, n=n):
                                    for k in range(KC):
                                        ins = pe.matmul(ps[:, bank, 0:n], lhsT=wv[:, k, mi * 128:(mi + 1) * 128], rhs=h_v[:, k, lc:lc + n], start=(k == 0), stop=(k == KC - 1))
                                    return ins
                                op("pe", mm, r=[wgb, hB[g % 2]], w=[PB[bg_]])
                                op("pe", lambda pe, wv=wuv, bank=bu_, mi=mi, lc=lc, n=n: mm(pe, wv, bank, mi, lc, n), r=[wub, hB[g % 2]], w=[PB[bu_]])
                                si = cntA[0] % 2
                                op("act", lambda a, bank=bg_, n=n, si=si: a.activation(out=sg[si][:, 0:n], in_=ps[:, bank, 0:n], func=AF.Silu), r=[PB[bg_]], w=[sgb[si]])
                                op("dve", lambda v, bank=bu_, n=n, si=si, m=m, lc=lc: v.tensor_tensor(out=a_v[:, m, lc:lc + n], in0=sg[si][:, 0:n], in1=ps[:, bank, 0:n], op=ALU.mult), r=[sgb[si], PB[bu_]], w=[aB[g % 2]])
                        if bg is not None:
                            steps(bg, bg_n)

                def phaseB(g):
                    loc = locs(g)
                    a_v, out_v = a_vs[g % 2], out_vs[g % 2]
                    for mo in range(KC):
                        wdv, wdb = ring_load(wd[:, mo * 128:(mo + 1) * 128].rearrange("(k p) n -> p k n", p=128), [FC, 128])
                        for ti in groups[g]:
                            n = tiles[ti][1]
                            lc = loc[ti]
                            bank = 4 + cntB[0] % 3
                            cntB[0] += 1

                            def mm2(pe, wdv=wdv, bank=bank, lc=lc, n=n):
                                for m in range(FC):
                                    ins = pe.matmul(ps[:, bank, 0:n], lhsT=wdv[:, m, :], rhs=a_v[:, m, lc:lc + n], start=(m == 0), stop=(m == FC - 1))
                                return ins
                            op("pe", mm2, r=[wdb, aB[g % 2]], w=[PB[bank]])
                            op("act", lambda a, bank=bank, n=n, mo=mo, lc=lc: a.copy(out=out_v[:, mo, lc:lc + n], in_=ps[:, bank, 0:n]), r=[PB[bank]], w=[hB[g % 2]])

                def post(g):
                    loc = locs(g)
                    a_v, out_v = a_vs[g % 2], out_vs[g % 2]
                    for ti in groups[g]:
                        n = tiles[ti][1]
                        lc = loc[ti]
                        sqv = a_v[:, 0:KC, lc:lc + n]
                        post_norm_update(sub, out_v[:, :, lc:lc + n], hB[g % 2], sqv, aB[g % 2], ti, S)
                    if g == 0:
                        tap(f"x_l{l}f{fi}", xT[:, 0, 0:512], [xB[0]])

                NG = len(groups)
                pre(0)
                phaseA(0)
                for g in range(NG):
                    if g + 1 < NG:
                        pre(g + 1)
                    phaseB(g)
                    if g + 1 < NG:
                        phaseA(g + 1)
                    post(g)
                all_engines_finish(scratch_bufs(S) + sgb + hB + aB)

        def out_proj_update(sub, wout, woutb, z_fn, zb_fn, tl, out_ts, out_tb, S):
            for jj, ti in enumerate(tl):
                n = tiles[ti][1]
                zv, zb = z_fn(ti), zb_fn(ti)
                ot, otb = out_ts[jj % len(out_ts)], out_tb[jj % len(out_ts)]
                for mo in range(KC):
                    bank = mo % 6

                    def mm(pe, mo=mo, bank=bank, zv=zv, n=n):
                        for k in range(KC):
                            ins = pe.matmul(ps[:, bank, 0:n], lhsT=wout[:, k, mo * 128:(mo + 1) * 128], rhs=zv[:, k, :], start=(k == 0), stop=(k == KC - 1))
                        return ins
                    op("pe", mm, r=[woutb, zb], w=[PB[bank]])
                    op("act", lambda a, mo=mo, bank=bank, n=n, ot=ot: a.copy(out=ot[:, mo, 0:n], in_=ps[:, bank, 0:n]), r=[PB[bank]], w=[otb])
                post_norm_update(sub, ot[:, :, 0:n], otb, zv, zb, ti, S)

        def conv_mixer(sub):
            with ExitStack() as s1:
                S = alloc_scratch(s1)
                GT = 1040
                hbuf = sb("c_h", [128, KC * GT], BF16, stack=s1)
                h_v = hbuf[:].rearrange("p (k t) -> p k t", t=GT)
                out_t = hbuf[:].bitcast(F32).rearrange("p (k t) -> p k t", t=GT // 2)
                hb = [Buf()] * 3
                z_v = sb("c_z", [128, KC, GT], BF16, stack=s1)
                zb = [Buf() for _ in range(3)]
                wout = sb("c_wout", [128, KC, D], BF16, stack=s1)
                woutb = Buf()
                ubuf = [sb(f"c_u{i}", [128, 2 + GT], stack=s1) for i in range(2)]
                ub = [Buf() for _ in range(2)]
                halo = sb("c_halo", [128, KC, 32], stack=s1)
                halob = Buf()
                cbufT = sb("c_cbufT", [128, KC, 32], stack=s1)
                cbufb = Buf()
                cst = sb("c_cst", [32, D], stack=s1)
                cstb = Buf()
                us_out = sb("c_us", [128, KC, NS], stack=s1)
                usb = Buf()
                cg_t = [sb(f"c_cg{i}", [128, 512], stack=s1) for i in range(2)]
                bg_t = [sb(f"c_bg{i}", [128, 512], stack=s1) for i in range(2)]
                cc_t = [sb(f"c_cc{i}", [128, 512], stack=s1) for i in range(2)]
                cgb = [Buf() for _ in range(2)]
                bgb = [Buf() for _ in range(2)]
                ccb = [Buf() for _ in range(2)]
                orow = sb("c_orow", [NS, D], stack=s1)
                orowb = Buf()
                dma("pool", wout[:], cv_w_out[0].rearrange("(k p) n -> p k n", p=128), w=[woutb])
                op("dve", lambda v: v.memset(halo[:], 0.0), w=[halob])
                dma("sp", cst[:], st_cv.rearrange("i j d -> (i j) d"), w=[cstb])
                dma("sp", cv_s[:, 0, :], st_cv[:, 1, :])

                def tr_cs(pe):
                    for k in range(KC):
                        ins = pe.transpose(ps[:, 6, k * 32:(k + 1) * 32], cst[:, k * 128:(k + 1) * 128], ident[0:32, 0:32])
                    return ins
                op("pe", tr_cs, r=[cstb, constb], w=[PB[6]])
                op("dve", lambda v: v.tensor_copy(out=cbufT[:], in_=ps[:, 6, 0:KC * 32].rearrange("p (k c) -> p k c", c=32)), r=[PB[6]], w=[cbufb])
                cw = lambda j, m: vC[:, j * 8 + m:j * 8 + m + 1]
                cnt = 0
                for g in range(2):
                    tl = [2 * g, 2 * g + 1] + ([4] if g == 1 else [])
                    loc = {ti: (j * 512) for j, ti in enumerate(tl)}
                    pre_norm(sub, lambda ti: h_v[:, :, loc[ti]:loc[ti] + tiles[ti][1]], lambda ti: hb, tl, S)
                    for m in range(KC):
                        si = ring_i[0] % RING_N
                        ring_i[0] += 1
                        wv = ring[:, si, 0:3 * KC * 128].rearrange("p (j k c) -> p k j c", j=3, k=KC)
                        wb = ringB[si]
                        for jj in range(3):
                            dma("pool", ring[:, si, jj * KC * 128:(jj + 1) * KC * 128].rearrange("p (k c) -> p k c", k=KC),
                                cv_w_in[0, :, jj * D + m * 128:jj * D + (m + 1) * 128].rearrange("(k p) c -> p k c", p=128), w=[wb])
                        u, ubb = ubuf[m % 2], ub[m % 2]
                        op("dve", lambda v, u=u, m=m: v.tensor_copy(out=u[:, 0:2], in_=halo[:, m, 0:2]), r=[halob], w=[ubb])
                        for j, ti in enumerate(tl):
                            n = tiles[ti][1]
                            lc = loc[ti]
                            b0 = 3 * (cnt % 2)
                            ci = cnt % 2
                            cnt += 1
                            for jj in range(3):
                                def mm(pe, jj=jj, wv=wv, b0=b0, lc=lc, n=n):
                                    for k in range(KC):
                                        ins = pe.matmul(ps[:, b0 + jj, 0:n], lhsT=wv[:, k, jj, :], rhs=h_v[:, k, lc:lc + n], start=(k == 0), stop=(k == KC - 1))
                                    return ins
                                op("pe", mm, r=[wb, hb[j]], w=[PB[b0 + jj]])
                            op("act", lambda a, ci=ci, b0=b0, n=n: a.copy(out=bg_t[ci][:, 0:n], in_=ps[:, b0, 0:n]), r=[PB[b0]], w=[bgb[ci]])
                            op("act", lambda a, ci=ci, b0=b0, n=n: a.copy(out=cg_t[ci][:, 0:n], in_=ps[:, b0 + 1, 0:n]), r=[PB[b0 + 1]], w=[cgb[ci]])
                            cc, cb = cc_t[ci], ccb[ci]
                            if ti < 4:
                                uc = u[:, 2 + lc:2 + lc + n]
                                op("dve", lambda v, uc=uc, ci=ci, b0=b0, n=n: v.tensor_tensor(out=uc, in0=cg_t[ci][:, 0:n], in1=ps[:, b0 + 2, 0:n], op=ALU.mult), r=[cgb[ci], PB[b0 + 2]], w=[ubb])
                                op("dve", lambda v, uc=uc, cc=cc, m=m, n=n: v.tensor_scalar(out=cc[:, 0:n], in0=uc, scalar1=cw(2, m), scalar2=None, op0=ALU.mult), r=[ubb, vCb], w=[cb])
                                op("dve", lambda v, u=u, cc=cc, m=m, n=n, lc=lc: v.scalar_tensor_tensor(out=cc[:, 0:n], in0=u[:, 1 + lc:1 + lc + n], scalar=cw(1, m), in1=cc[:, 0:n], op0=ALU.mult, op1=ALU.add), r=[ubb, vCb, cb], w=[cb])
                                op("dve", lambda v, u=u, cc=cc, m=m, n=n, lc=lc: v.scalar_tensor_tensor(out=cc[:, 0:n], in0=u[:, lc:lc + n], scalar=cw(0, m), in1=cc[:, 0:n], op0=ALU.mult, op1=ALU.add), r=[ubb, vCb, cb], w=[cb])
                            else:
                                uc = us_out[:, m, :]
                                b01 = cbufT[:, m, :].rearrange("p (i j) -> p i j", j=2)
                                op("dve", lambda v, uc=uc, ci=ci, b0=b0, n=n: v.tensor_tensor(out=uc, in0=cg_t[ci][:, 0:n], in1=ps[:, b0 + 2, 0:n], op=ALU.mult), r=[cgb[ci], PB[b0 + 2]], w=[usb])
                                op("dve", lambda v, uc=uc, cc=cc, m=m, n=n: v.tensor_scalar(out=cc[:, 0:n], in0=uc, scalar1=cw(2, m), scalar2=None, op0=ALU.mult), r=[usb, vCb], w=[cb])
                                op("dve", lambda v, cc=cc, m=m, n=n, b01=b01: v.scalar_tensor_tensor(out=cc[:, 0:n], in0=b01[:, :, 1], scalar=cw(1, m), in1=cc[:, 0:n], op0=ALU.mult, op1=ALU.add), r=[cbufb, vCb, cb], w=[cb])
                                op("dve", lambda v, cc=cc, m=m, n=n, b01=b01: v.scalar_tensor_tensor(out=cc[:, 0:n], in0=b01[:, :, 0], scalar=cw(0, m), in1=cc[:, 0:n], op0=ALU.mult, op1=ALU.add), r=[cbufb, vCb, cb], w=[cb])
                            op("dve", lambda v, cc=cc, ci=ci, m=m, n=n, lc=lc: v.tensor_tensor(out=z_v[:, m, lc:lc + n], in0=cc[:, 0:n], in1=bg_t[ci][:, 0:n], op=ALU.mult), r=[cb, bgb[ci]], w=[zb[j]])
                        op("dve", lambda v, u=u, m=m: v.tensor_copy(out=halo[:, m, 0:2], in_=u[:, 1024:1026]), r=[ubb], w=[halob])
                    if g == 0:
                        tap("cv_z", z_v[:, 0, 0:512], [zb[0]])
                    out_proj_update(sub, wout, woutb, lambda ti: z_v[:, :, loc[ti]:loc[ti] + tiles[ti][1]], lambda ti: zb[tl.index(ti)], tl, [out_t], [hb[0]], S)
                    for e_ in ("pe", "act", "dve"):
                        K.finish(e_, hb)
                def tr_h(pe):
                    for k in range(KC):
                        ins = pe.transpose(ps[0:32, k // 4, (k % 4) * 128:(k % 4 + 1) * 128], halo[:, k, :], ident[:])
                    return ins
                op("pe", tr_h, r=[halob, constb], w=[PB[0], PB[1]])
                op("dve", lambda v: v.tensor_copy(out=orow[0:2, :], in_=ps[0:2, 0:2, :].rearrange("p b t -> p (b t)")), r=[PB[0], PB[1]], w=[orowb])
                dma("sp", cv_p, orow[0:2, :], r=[orowb])

                def tr_u(pe):
                    for k in range(KC):
                        ins = pe.transpose(ps[0:NS, 2 + k // 4, (k % 4) * 128:(k % 4 + 1) * 128], us_out[:, k, :], ident[:])
                    return ins
                op("pe", tr_u, r=[usb, constb], w=[PB[2], PB[3]])
                op("dve", lambda v: v.tensor_copy(out=orow[:, :], in_=ps[0:NS, 2:4, :].rearrange("p b t -> p (b t)")), r=[PB[2], PB[3]], w=[orowb])
                dma("sp", cv_s[:, 1, :], orow[:, :], r=[orowb])
                all_engines_finish(scratch_bufs(S) + hb + zb + [woutb, halob, cbufb, cstb, usb, orowb] + ub + cgb + bgb + ccb)

        def ml_fin_a(n, groups, den_ap, denb, fl_ap, flb, so_ap, sob, yt, ytb, ybf, ybfb, sm8, sm8b):
            den, dn, rdn, ss, t1, scal = [sm8[0:n, i, :] for i in range(6)]
            op("dve", lambda v: v.tensor_tensor(out=dn, in0=den_ap, in1=fl_ap, op=ALU.max), r=[denb, flb], w=[sm8b])
            op("dve", lambda v: v.reciprocal(out=rdn, in_=dn), r=[sm8b], w=[sm8b])
            for (ap, h0, g, bufs) in groups:
                op("act", lambda a, ap=ap, h0=h0, g=g: a.activation(out=yt[0:n, h0:h0 + g, :], in_=ap, func=AF.Square), r=bufs, w=[ytb])
            op("dve", lambda v: v.tensor_reduce(out=ss, in_=yt[0:n, :, :], axis=AX.X, op=ALU.add), r=[ytb], w=[sm8b])
            op("dve", lambda v: v.tensor_tensor(out=t1, in0=rdn, in1=rdn, op=ALU.mult), r=[sm8b], w=[sm8b])
            op("dve", lambda v: v.tensor_tensor(out=t1, in0=t1, in1=ss, op=ALU.mult), r=[sm8b], w=[sm8b])
            op("act", lambda a: a.activation(out=t1, in_=t1, func=AF.Sqrt, bias=eps_t[0:n, 0:1], scale=1.0 / DV), r=[sm8b, constb], w=[sm8b])
            op("dve", lambda v: v.reciprocal(out=t1, in_=t1), r=[sm8b], w=[sm8b])
            op("dve", lambda v: v.tensor_tensor(out=scal, in0=rdn, in1=t1, op=ALU.mult), r=[sm8b], w=[sm8b])
            for (ap, h0, g, bufs) in groups:
                op("dve", lambda v, ap=ap, h0=h0, g=g: v.tensor_tensor(out=yt[0:n, h0:h0 + g, :], in0=ap, in1=scal[:, h0:h0 + g, None].broadcast_to([n, g, DV]), op=ALU.mult), r=bufs + [sm8b], w=[ytb])
            op("dve", lambda v: v.tensor_tensor(out=ybf[0:n, :, :], in0=yt[0:n, :, :], in1=so_ap, op=ALU.mult), r=[ytb, sob], w=[ybfb])

        def ml_fin_b(n, ybf, ybfb, col0, hyw):
            psb7 = ps[:, 7, :].bitcast(BF16)

            def tr(pe):
                for k in range(H):
                    ins = pe.transpose(psb7[:, k * 128:k * 128 + n], ybf[0:n, k, :], identb[0:n, 0:n])
                return ins
            op("pe", tr, r=[ybfb, constb], w=[PB[7]])
            src = psb7.rearrange("p (k t) -> p k t", t=128)[:, :, 0:n]
            op("dve", lambda v: v.tensor_tensor(out=hy_ref[0][:, :, col0:col0 + n], in0=src, in1=vC[:, 24:32, None].broadcast_to([128, H, n]), op=ALU.mult), r=[PB[7], vCb], w=hyw)

        hy_ref = [None]

        def mlstm_mixer(sub):
            w_in = ml_w_in[0]
            with ExitStack() as s1:
                hy = sb("m_hy", [128, KC, TT], BF16, stack=s1)
                hy_ref[0] = hy
                hyB = [Buf() for _ in range(NBLK + 1)]
                voTs = sb("m_voTs", [128, 16, NS], stack=s1)
                qkTs = sb("m_qkTs", [128, 8, NS], BF16, stack=s1)
                gs_rows = sb("m_gsrows", [8, 4, NS], stack=s1)
                gs_tok = sb("m_gstok", [NS, 4, H], stack=s1)
                voTsb, qkTsb, gsrb, gstb = Buf(), Buf(), Buf(), Buf()
                with ExitStack() as s2:
                    S = alloc_scratch(s2)
                    pre_norm(sub, lambda ti: hy[:, :, tiles[ti][0]:tiles[ti][0] + tiles[ti][1]],
                             lambda ti: (hyB[4 * ti:4 * ti + 4] if ti < 4 else [hyB[NBLK]]), [0, 1, 2, 3, 4], S)
                    tap("hm", hy[:, 0, 0:512], hyB[0:4])
                    all_engines_finish(scratch_bufs(S))
                stage(1)
                with ExitStack() as s2:
                    wqk = sb("m_wqk", [128, KC, 1024], BF16, stack=s2)
                    wif = sb("m_wif", [128, KC, 16], BF16, stack=s2)
                    qkT = sb("m_qkT", [128, 8, 512], BF16, stack=s2)
                    ktok = sb("m_ktok", [128, 512], BF16, stack=s2)
                    vaug = sb("m_vaug", [128, H, AUG], BF16, stack=s2)
                    so2 = [sb(f"m_so{i}", [128, H, DV], stack=s2) for i in range(2)]
                    so2b = [Buf(), Buf()]
                    ybf = sb("m_ybf", [128, H, DV], BF16, stack=s2)
                    ybfb = Buf()
                    Sm = sb("m_Sm", [128, H, 128], BF16, stack=s2)
                    yt = sb("m_yt", [128, H, DV], stack=s2)
                    Cst = sb("m_Cst", [128, 4, AUG], stack=s2)
                    Cbf = sb("m_Cbf", [128, 4, AUG], BF16, stack=s2)
                    g_th = sb("m_gth", [8, 512], stack=s2)
                    g_z = sb("m_gz", [8, 512], stack=s2)
                    g_mz = sb("m_gmz", [8, 512], stack=s2)
                    Mp = sb("m_Mp", [8, 640], stack=s2)
                    ones8 = sb("m_ones8", [8, 512], stack=s2)
                    onesf = sb("m_onesf", [8, 128], stack=s2)
                    sm1 = sb("m_sm1", [8, 64], stack=s2)
                    decr = sb("m_decr", [8, 4, 8], stack=s2)
                    ef_tok = sb("m_ef", [128, NBLK, 2, H], stack=s2)
                    decb = sb("m_decb", [128, NBLK, H], stack=s2)
                    sm8 = sb("m_sm8", [128, 6, H], stack=s2)
                    wqkb, wifb, qkTb, ktokb, vaugb, Smb, ytb, Cstb, Cbfb = [Buf() for _ in range(9)]
                    gthb, gzb, gmzb, Mpb, o8b, sm1b, decrb, efb, decbb, sm8b = [Buf() for _ in range(10)]
                    wvo_s = [ring[:, i, :].rearrange("p (k n) -> p k n", k=KC) for i in range(4)]
                    dma("pool", wqk[:], w_in[:, 0:1024].rearrange("(k p) n -> p k n", p=128), w=[wqkb])
                    dma("pool", wif[:], w_in[:, 3072:3088].rearrange("(k p) n -> p k n", p=128), w=[wifb])
                    for i in range(4):
                        dma("pool", wvo_s[i], w_in[:, 1024 + i * 512:1024 + (i + 1) * 512].rearrange("(k p) n -> p k n", p=128), w=[ringB[i]])
                    ring_i[0] = 0
                    dma("sp", sm1[:, 2:3], ml_b_i.rearrange("o h -> h o"), w=[sm1b])
                    dma("sp", sm1[:, 3:4], ml_b_f.rearrange("o h -> h o"), w=[sm1b])
                    dma("sp", sm1[:, 16:32], st_m.rearrange("i h -> h i"), w=[sm1b], allow_slow_non_contiguous=True)
                    op("dve", lambda v: v.tensor_scalar(out=sm1[:, 2:3], in0=sm1[:, 2:3], scalar1=1.0 / CAP, scalar2=None, op0=ALU.mult), r=[sm1b], w=[sm1b])
                    op("dve", lambda v: v.memset(sm1[:, 0:2], 0.0), w=[sm1b])
                    op("dve", lambda v: v.memset(ones8[:], 1.0), w=[o8b])
                    op("dve", lambda v: v.memset(onesf[:], 1.0), w=[o8b])
                    op("dve", lambda v: v.memset(Cst[:], 0.0), w=[Cstb])
                    op("dve", lambda v: v.memset(Cbf[:], 0.0), w=[Cbfb])
                    op("dve", lambda v: v.memset(vaug[:], 0.0), w=[vaugb])
                    op("dve", lambda v: v.memset(Mp[:], 0.0), w=[Mpb])
                    stage(2)

                    def gate_pre(c0, n, hbufs):
                        def mmg(pe, col):
                            for k in range(KC):
                                ins = pe.matmul(ps[0:8, 7, 0:n], lhsT=wif[:, k, col:col + 8], rhs=hy[:, k, c0:c0 + n], start=(k == 0), stop=(k == KC - 1))
                            return ins
                        op("pe", lambda pe: mmg(pe, 0), r=[wifb] + hbufs, w=[PB[7]])
                        op("act", lambda a: a.activation(out=g_th[:, 0:n], in_=ps[0:8, 7, 0:n], func=AF.Tanh, bias=sm1[:, 2:3], scale=1.0 / CAP), r=[PB[7], sm1b], w=[gthb])
                        op("pe", lambda pe: mmg(pe, 8), r=[wifb] + hbufs, w=[PB[7]])
                        op("dve", lambda v: v.tensor_scalar(out=g_z[:, 0:n], in0=ps[0:8, 7, 0:n], scalar1=sm1[:, 3:4], scalar2=None, op0=ALU.add), r=[PB[7], sm1b], w=[gzb])
                        op("dve", lambda v: v.tensor_scalar(out=g_mz[:, 0:n], in0=g_z[:, 0:n], scalar1=0.0, scalar2=None, op0=ALU.min), r=[gzb], w=[gmzb])
                        op("act", lambda a: a.activation(out=g_z[:, 0:n], in_=g_z[:, 0:n], func=AF.Abs), r=[gzb], w=[gzb])
                        op("act", lambda a: a.activation(out=g_z[:, 0:n], in_=g_z[:, 0:n], func=AF.Exp, scale=-1.0), r=[gzb], w=[gzb])
                        op("act", lambda a: a.activation(out=g_z[:, 0:n], in_=g_z[:, 0:n], func=AF.Ln, bias=1.0), r=[gzb], w=[gzb])
                        op("dve", lambda v: v.tensor_tensor(out=g_mz[:, 0:n], in0=g_mz[:, 0:n], in1=g_z[:, 0:n], op=ALU.subtract), r=[gzb, gmzb], w=[gmzb])

                    def qk_proj(c0, n, hbufs, dst, dstb):
                        for j in range(8):
                            def mm(pe, j=j):
                                for k in range(KC):
                                    ins = pe.matmul(ps[:, 7, 0:n], lhsT=wqk[:, k, j * 128:(j + 1) * 128], rhs=hy[:, k, c0:c0 + n], start=(k == 0), stop=(k == KC - 1))
                                return ins
                            op("pe", mm, r=[wqkb] + hbufs, w=[PB[7]])
                            if j < 4:
                                op("act", lambda a, j=j: a.activation(out=dst[:, j, 0:n], in_=ps[:, 7, 0:n], func=AF.Copy, scale=DQK ** -0.5), r=[PB[7]], w=[dstb])
                            else:
                                op("dve", lambda v, j=j: v.tensor_copy(out=dst[:, j, 0:n], in_=ps[:, 7, 0:n]), r=[PB[7]], w=[dstb])

                    hs_b = [hyB[NBLK]]
                    qk_proj(T, NS, hs_b, qkTs, qkTsb)
                    for nb in range(4):
                        for mi in range(4):
                            def mm(pe, nb=nb, mi=mi):
                                for k in range(KC):
                                    ins = pe.matmul(ps[:, 7, 0:NS], lhsT=wvo_s[nb][:, k, mi * 128:(mi + 1) * 128], rhs=hy[:, k, T:TT], start=(k == 0), stop=(k == KC - 1))
                                return ins
                            op("pe", mm, r=[ringB[nb]] + hs_b, w=[PB[7]])
                            op("dve", lambda v, nb=nb, mi=mi: v.tensor_copy(out=voTs[:, nb * 4 + mi, :], in_=ps[:, 7, 0:NS]), r=[PB[7]], w=[voTsb])
                            stage(3)
                    gate_pre(T, NS, hs_b)
                    igs, lfm = sm1[:, 32:48], g_mz[:, 0:NS]
                    op("dve", lambda v: v.tensor_scalar(out=igs, in0=g_th[:, 0:NS], scalar1=CAP, scalar2=None, op0=ALU.mult), r=[gthb], w=[sm1b])
                    op("dve", lambda v: v.tensor_tensor(out=lfm, in0=lfm, in1=sm1[:, 16:32], op=ALU.add), r=[gmzb, sm1b], w=[gmzb])
                    op("dve", lambda v: v.tensor_tensor(out=gs_rows[:, 0, :], in0=lfm, in1=igs, op=ALU.max), r=[gmzb, sm1b], w=[gsrb])
                    op("dve", lambda v: v.tensor_tensor(out=gs_rows[:, 1, :], in0=lfm, in1=gs_rows[:, 0, :], op=ALU.subtract), r=[gmzb, gsrb], w=[gsrb])
                    op("dve", lambda v: v.tensor_tensor(out=gs_rows[:, 2, :], in0=igs, in1=gs_rows[:, 0, :], op=ALU.subtract), r=[sm1b, gsrb], w=[gsrb])
                    op("act", lambda a: a.activation(out=gs_rows[:, 1:3, :], in_=gs_rows[:, 1:3, :], func=AF.Exp), r=[gsrb], w=[gsrb])
                    op("act", lambda a: a.activation(out=gs_rows[:, 3, :], in_=gs_rows[:, 0, :], func=AF.Exp, scale=-1.0), r=[gsrb], w=[gsrb])

                    def tr_gs(pe):
                        for q in range(4):
                            ins = pe.transpose(ps[0:NS, 7, q * 8:(q + 1) * 8], gs_rows[:, q, :], ident[0:8, 0:8])
                        return ins
                    op("pe", tr_gs, r=[gsrb, constb], w=[PB[7]])
                    op("dve", lambda v: v.tensor_copy(out=gs_tok[:].rearrange("p a b -> p (a b)"), in_=ps[0:NS, 7, 0:32]), r=[PB[7]], w=[gstb])
                    stage(4)

                    psb7k = ps[:, 7, 0:256].bitcast(BF16)

                    def stage1(b):
                        c0 = (b % 4) * 128
                        hyb = hyB[b]
                        so, sob = so2[b % 2], so2b[b % 2]
                        for nb in range(4):
                            def mm(pe, nb=nb):
                                for k in range(KC):
                                    ins = pe.matmul(ps[:, nb, :], lhsT=hy[:, k, b * 128:(b + 1) * 128], rhs=wvo_s[nb][:, k, :], start=(k == 0), stop=(k == KC - 1))
                                return ins
                            op("pe", mm, r=[hyb, ringB[nb]], w=[PB[nb]])

                        def tr_k(pe):
                            for j in range(4):
                                ins = pe.transpose(psb7k[:, j * 128:(j + 1) * 128], qkT[:, 4 + j, c0:c0 + 128], identb[:])
                            return ins
                        op("pe", tr_k, r=[qkTb, constb], w=[PB[7]])
                        op("act", lambda a: a.copy(out=ktok[:], in_=psb7k), r=[PB[7]], w=[ktokb])
                        op("dve", lambda v: v.tensor_tensor(out=vaug[:, :, 0:DV], in0=ps[:, 0:2, :].rearrange("p b (h e) -> p (b h) e", e=DV), in1=ef_tok[:, b, 0, :, None].broadcast_to([128, H, DV]), op=ALU.mult), r=[PB[0], PB[1], efb], w=[vaugb])
                        op("dve", lambda v: v.tensor_copy(out=vaug[:, :, DV:DV + 1], in_=ef_tok[:, b, 0, :, None]), r=[efb], w=[vaugb])
                        op("act", lambda a: a.activation(out=so[:], in_=ps[:, 2:4, :].rearrange("p b (h e) -> p (b h) e", e=DV), func=AF.Sigmoid), r=[PB[2], PB[3]], w=[sob])

                        def mm_s(pe):
                            for hd in range(H):
                                po = (hd % 2) * 64
                                ins = pe.matmul(ps[:, hd % 2, (hd // 2) * 128:(hd // 2 + 1) * 128], lhsT=qkT[po:po + 64, 4 + hd // 2, c0:c0 + 128], rhs=qkT[po:po + 64, hd // 2, c0:c0 + 128], start=True, stop=True)
                            return ins
                        op("pe", mm_s, r=[qkTb], w=[PB[0], PB[1]])
                        op("dve", lambda v: v.tensor_tensor(out=Sm[:], in0=ps[:, 0:2, :].rearrange("p b (h t) -> p (b h) t", t=128), in1=cmask[:, None, :].broadcast_to([128, H, 128]), op=ALU.mult), r=[PB[0], PB[1], constb], w=[Smb])
                        if b == 0:
                            stage(6)

                    def stage2(b):
                        c0 = (b % 4) * 128
                        so, sob = so2[b % 2], so2b[b % 2]

                        def mm_n(pe):
                            for hd in range(H):
                                po = (hd % 2) * 64
                                o_ap = ps[:, 4 + hd // 3, (hd % 3) * AUG:(hd % 3) * AUG + NA]
                                pe.matmul(o_ap, lhsT=qkT[po:po + 64, hd // 2, c0:c0 + 128], rhs=Cbf[po:po + 64, hd // 2, 0:NA], start=True, stop=False)
                                ins = pe.matmul(o_ap, lhsT=Sm[:, (hd % 2) * 4 + hd // 2, :], rhs=vaug[:, hd, 0:NA], start=False, stop=True)
                            return ins
                        op("pe", mm_n, r=[qkTb, Cbfb, Smb, vaugb], w=[PB[4], PB[5], PB[6]])

                        def mm_d(pe):
                            for hd in range(H):
                                po = (hd % 2) * 64
                                j = hd // 2
                                o_ap = ps[po:po + 64, 2, j * AUG:j * AUG + NA] if j < 3 else ps[po:po + 64, 3, 0:NA]
                                ins = pe.matmul(o_ap, lhsT=ktok[:, hd * 64:(hd + 1) * 64], rhs=vaug[:, hd, 0:NA], start=True, stop=True)
                            return ins
                        op("pe", mm_d, r=[ktokb, vaugb], w=[PB[2], PB[3]])
                        op("dve", lambda v: v.tensor_tensor(out=Cst[:, 0:3, 0:NA], in0=Cst[:, 0:3, 0:NA], in1=ps[:, 2, 0:3 * AUG].rearrange("p (j e) -> p j e", j=3)[:, :, 0:NA], op=ALU.add), r=[PB[2], Cstb], w=[Cstb])
                        op("dve", lambda v: v.tensor_tensor(out=Cst[:, 3, 0:NA], in0=Cst[:, 3, 0:NA], in1=ps[:, 3, 0:NA], op=ALU.add), r=[PB[3], Cstb], w=[Cstb])
                        for half in range(2):
                            po = half * 64
                            op("dve", lambda v, po=po, half=half: v.tensor_tensor(out=Cst[po:po + 64, :, 0:NA], in0=Cst[po:po + 64, :, 0:NA], in1=decb[po:po + 64, b, half::2][:, :, None].broadcast_to([64, 4, NA]), op=ALU.mult), r=[Cstb, decbb], w=[Cstb])
                        op("act", lambda a: a.copy(out=Cbf[:], in_=Cst[:]), r=[Cstb], w=[Cbfb])
                        if b == 0:
                            stage(7)
                        groups = []
                        for gi, (h0, g) in enumerate(((0, 3), (3, 3), (6, 2))):
                            na = ps[:, 4 + gi, 0:g * AUG].rearrange("p (h e) -> p h e", e=AUG)
                            groups.append((na[:, :, 0:DV], h0, g, [PB[4 + gi]]))
                            op("act", lambda a, na=na, h0=h0, g=g: a.activation(out=sm8[:, 0, h0:h0 + g], in_=na[:, :, DV], func=AF.Abs), r=[PB[4 + gi]], w=[sm8b])
                        ml_fin_a(128, groups, sm8[:, 0, :], sm8b, ef_tok[:, b, 1, :], efb, so[:], sob, yt, ytb, ybf, ybfb, sm8, sm8b)

                    def stage3(b):
                        ml_fin_b(128, ybf, ybfb, b * 128, [hyB[b]])
                        if b == 0:
                            tap("ml_y0", hy[:, 0, 0:128], [hyB[0]])
                            stage(8)

                    for tt in range(4):
                        c0t = tt * 512
                        hbt = hyB[4 * tt:4 * tt + 4]
                        qk_proj(c0t, 512, hbt, qkT, qkTb)
                        gate_pre(c0t, 512, hbt)
                        op("dve", lambda v: v.tensor_tensor_scan(out=g_z[:], data0=ones8[:], data1=g_mz[:], initial=sm1[:, 0:1], op0=ALU.mult, op1=ALU.add), r=[gmzb, o8b, sm1b], w=[gzb])
                        op("dve", lambda v: v.scalar_tensor_tensor(out=g_th[:], in0=g_th[:], scalar=CAP, in1=g_z[:], op0=ALU.mult, op1=ALU.subtract), r=[gthb, gzb], w=[gthb])
                        op("dve", lambda v: v.tensor_copy(out=Mp[:, 127:128], in_=sm1[:, 1:2]), r=[sm1b], w=[Mpb])
                        op("dve", lambda v: v.tensor_tensor_scan(out=Mp[:, 128:640], data0=ones8[:], data1=g_th[:], initial=sm1[:, 1:2], op0=ALU.mult, op1=ALU.max), r=[gthb, o8b, sm1b, Mpb], w=[Mpb])
                        op("dve", lambda v: v.tensor_copy(out=sm1[:, 0:1], in_=g_z[:, 511:512]), r=[gzb], w=[sm1b])
                        op("dve", lambda v: v.tensor_copy(out=sm1[:, 1:2], in_=Mp[:, 639:640]), r=[Mpb], w=[sm1b])
                        if tt == 3:
                            op("dve", lambda v: v.tensor_tensor(out=sm1[:, 48:49], in0=g_z[:, 511:512], in1=Mp[:, 639:640], op=ALU.add), r=[gzb, Mpb], w=[sm1b])
                            op("dve", lambda v: v.memset(decr[:], 0.0), w=[decrb])
                            op("dve", lambda v: v.tensor_copy(out=decr[:, 0, 0:1], in_=sm1[:, 48:49]), r=[sm1b, decrb], w=[decrb])
                            op("pe", lambda pe: pe.transpose(ps[0:32, 7, 128:136], decr[:].rearrange("p a b -> p (a b)"), ident[0:8, 0:8]), r=[decrb, constb], w=[PB[7]])
                            op("dve", lambda v: v.tensor_copy(out=sm8[0:32, 5, :], in_=ps[0:32, 7, 128:136]), r=[PB[7]], w=[sm8b])
                            dma("sp", m_p.rearrange("h o -> o h"), sm8[0:1, 5, :], r=[sm8b])
                        Mp5 = Mp[:].rearrange("p (b j) -> p b j", j=128)
                        mref = Mp5[:, 0:4, 127:128]
                        op("dve", lambda v: v.tensor_tensor(out=sm1[:, 4:8], in0=Mp5[:, 0:4, 127], in1=Mp5[:, 1:5, 127], op=ALU.subtract), r=[Mpb], w=[sm1b])
                        op("act", lambda a: a.activation(out=sm1[:, 4:8], in_=sm1[:, 4:8], func=AF.Exp), r=[sm1b], w=[sm1b])
                        a4 = g_th[:].rearrange("p (b j) -> p b j", j=128)
                        b4 = g_z[:].rearrange("p (b j) -> p b j", j=128)
                        op("dve", lambda v: v.tensor_tensor(out=a4, in0=a4, in1=mref.broadcast_to([8, 4, 128]), op=ALU.subtract), r=[gthb, Mpb], w=[gthb])
                        op("act", lambda a: a.activation(out=g_th[:], in_=g_th[:], func=AF.Exp), r=[gthb], w=[gthb])
                        op("dve", lambda v: v.scalar_tensor_tensor(out=b4, in0=b4, scalar=-1.0, in1=mref.broadcast_to([8, 4, 128]), op0=ALU.mult, op1=ALU.subtract), r=[gzb, Mpb], w=[gzb])
                        op("act", lambda a: a.activation(out=g_z[:], in_=g_z[:], func=AF.Exp), r=[gzb], w=[gzb])

                        def tr_ef(pe):
                            for bl in range(4):
                                pe.transpose(ps[:, 7, (bl * 2) * 8:(bl * 2 + 1) * 8], g_th[:, bl * 128:(bl + 1) * 128], ident[0:8, 0:8])
                                ins = pe.transpose(ps[:, 7, (bl * 2 + 1) * 8:(bl * 2 + 2) * 8], g_z[:, bl * 128:(bl + 1) * 128], ident[0:8, 0:8])
                            return ins
                        op("pe", tr_ef, r=[gthb, gzb, constb], w=[PB[7]])
                        op("dve", lambda v, tt=tt: v.tensor_copy(out=ef_tok[:, 4 * tt:4 * tt + 4, :, :].rearrange("p a b c -> p (a b c)"), in_=ps[:, 7, 0:64]), r=[PB[7]], w=[efb])
                        op("dve", lambda v: v.tensor_tensor(out=decr[:], in0=sm1[:, 4:8, None].broadcast_to([8, 4, 8]), in1=ident[0:8, None, 0:8].broadcast_to([8, 4, 8]), op=ALU.mult), r=[sm1b, constb], w=[decrb])
                        op("pe", lambda pe: pe.matmul(ps[:, 7, 64:96], lhsT=onesf[:], rhs=decr[:].rearrange("p a b -> p (a b)"), start=True, stop=True), r=[o8b, decrb], w=[PB[7]])
                        op("dve", lambda v, tt=tt: v.tensor_copy(out=decb[:, 4 * tt:4 * tt + 4, :].rearrange("p a b -> p (a b)"), in_=ps[:, 7, 64:96]), r=[PB[7]], w=[decbb])
                        stage(5)

                        for bl in range(4):
                            b = 4 * tt + bl
                            stage1(b)
                            if b > 0:
                                stage3(b - 1)
                            stage2(b)
                    stage3(NBLK - 1)
                    for half in range(2):
                        po = half * 64
                        dma("sp", C_p.rearrange("(j h) d e -> h d j e", h=2)[half], Cst[po:po + 64, :, 0:DV], r=[Cstb])
                    op("dve", lambda v: v.memset(yt[:, 0, 0:32], 0.0), w=[ytb])
                    op("dve", lambda v: v.tensor_copy(out=yt[:, 0, 0:4], in_=Cst[:, :, DV]), r=[Cstb], w=[ytb])
                    op("pe", lambda pe: pe.transpose(ps[0:32, 7, 0:128], yt[:, 0, 0:32], ident[:]), r=[ytb, constb], w=[PB[7]])
                    op("dve", lambda v: v.tensor_copy(out=so2[0][0:32, 0, :], in_=ps[0:32, 7, 0:128]), r=[PB[7]], w=[so2b[0]])
                    dma("sp", n_p.rearrange("(j h) d -> j (h d)", h=2), so2[0][0:4, 0, :], r=[so2b[0]])
                    all_engines_finish([wqkb, wifb, qkTb, ktokb, vaugb, ybfb, Smb, ytb, Cstb, Cbfb, gthb, gzb, gmzb, Mpb, o8b, sm1b, decrb, efb, decbb, sm8b] + ringB + so2b)
                    stage(9)
                with ExitStack() as s2:
                    v_tok = sb("s_vtok", [NS, H, DV], stack=s2)
                    so_tok = sb("s_sotok", [NS, H, DV], stack=s2)
                    qk_tok = sb("s_qktok", [NS, 2, H, DQK], stack=s2)
                    kbf = sb("s_kbf", [NS, H * DQK], BF16, stack=s2)
                    numC = sb("s_numC", [NS, H, DV], stack=s2)
                    vw = sb("s_vw", [NS, H, DV], stack=s2)
                    yts = sb("s_yt", [NS, H, DV], stack=s2)
                    ybfs = sb("s_ybf", [NS, H, DV], BF16, stack=s2)
                    ybfsb = Buf()
                    n0t = sb("s_n0t", [NS, H, DQK], stack=s2)
                    prod = sb("s_prod", [NS, H, DQK], stack=s2)
                    sm8s = sb("s_sm8", [NS, 10, H], stack=s2)
                    C0j = [sb(f"s_C0j{i}", [128, NS, DV], stack=s2) for i in range(2)]
                    wstb = sb("s_wstb", [128, 4, NS], stack=s2)
                    Psel = sb("s_Psel", [8, 128], stack=s2)
                    Dsel = sb("s_Dsel", [8, 8], stack=s2)
                    Wd = sb("s_Wd", [8, 4, NS], stack=s2)
                    vtb, sotb, qktb, kbfb, numCb, vwb, ytsb, n0b, prodb, sm8sb, wstbb, selb, Wdb = [Buf() for _ in range(13)]
                    C0jb = [Buf(), Buf()]
                    C0bf = [ring[:, i, 0:NS * DV].rearrange("p (i e) -> p i e", e=DV) for i in range(2)]
                    Vbd = ring[0:NS, 2, 0:NS * DV].rearrange("p (i e) -> p i e", e=DV)
                    tmpd = ring[0:NS, 3, :].bitcast(F32).rearrange("p (i e) -> p i e", e=DV)
                    mt_t, wst_t, wi_t, fls_t = [gs_tok[:, q, :] for q in range(4)]
                    dma("sp", n0t[:], st_n, w=[n0b])
                    op("dve", lambda v: v.tensor_reduce(out=Dsel[:, 4:5], in_=ident[0:8, 0:8:2], axis=AX.X, op=ALU.add), r=[constb], w=[selb])
                    op("dve", lambda v: v.tensor_reduce(out=Dsel[:, 5:6], in_=ident[0:8, 1:8:2], axis=AX.X, op=ALU.add), r=[constb], w=[selb])
                    op("dve", lambda v: v.tensor_copy(out=Psel[:, 0:64], in_=Dsel[:, 4:5].broadcast_to([8, 64])), r=[selb], w=[selb])
                    op("dve", lambda v: v.tensor_copy(out=Psel[:, 64:128], in_=Dsel[:, 5:6].broadcast_to([8, 64])), r=[selb], w=[selb])
                    op("dve", lambda v: v.tensor_tensor(out=Dsel[:, 0:4], in0=ident[0:8, 0:8:2], in1=ident[0:8, 1:8:2], op=ALU.add), r=[constb], w=[selb])
                    op("dve", lambda v: v.tensor_tensor(out=Wd[:], in0=gs_rows[:, 1, None, :].broadcast_to([8, 4, NS]), in1=Dsel[:, 0:4, None].broadcast_to([8, 4, NS]), op=ALU.mult), r=[gsrb, selb], w=[Wdb])
                    op("pe", lambda pe: pe.matmul(ps[:, 7, 0:4 * NS], lhsT=Psel[:], rhs=Wd[:].rearrange("p a b -> p (a b)"), start=True, stop=True), r=[selb, Wdb], w=[PB[7]])
                    op("dve", lambda v: v.tensor_copy(out=wstb[:].rearrange("p a b -> p (a b)"), in_=ps[:, 7, 0:4 * NS]), r=[PB[7]], w=[wstbb])
                    def tr_v(pe, base, bank0):
                        for c in range(8):
                            ins = pe.transpose(ps[0:NS, bank0 + c // 4, (c % 4) * 128:(c % 4 + 1) * 128], voTs[:, base + c, :], ident[:])
                        return ins
                    op("pe", lambda pe: tr_v(pe, 0, 0), r=[voTsb, constb], w=[PB[0], PB[1]])
                    op("dve", lambda v: v.tensor_copy(out=v_tok[:], in_=ps[0:NS, 0:2, :].rearrange("p b (h e) -> p (b h) e", e=DV)), r=[PB[0], PB[1]], w=[vtb])
                    op("pe", lambda pe: tr_v(pe, 8, 2), r=[voTsb, constb], w=[PB[2], PB[3]])
                    op("act", lambda a: a.activation(out=so_tok[:], in_=ps[0:NS, 2:4, :].rearrange("p b (h e) -> p (b h) e", e=DV), func=AF.Sigmoid), r=[PB[2], PB[3]], w=[sotb])
                    psb4 = ps[:, 4, :].bitcast(BF16)

                    def tr_qk(pe):
                        for c in range(8):
                            ins = pe.transpose(psb4[0:NS, c * 128:(c + 1) * 128], qkTs[:, c, :], identb[:])
                        return ins
                    op("pe", tr_qk, r=[qkTsb, constb], w=[PB[4]])
                    op("dve", lambda v: v.tensor_copy(out=qk_tok[:].rearrange("p a h d -> p (a h d)"), in_=psb4[0:NS, :]), r=[PB[4]], w=[qktb])
                    op("act", lambda a: a.copy(out=kbf[:], in_=psb4[0:NS, 512:1024]), r=[PB[4]], w=[kbfb])
                    qkd, qn, s_, den = [sm8s[:, i, :] for i in (6, 7, 8, 9)]
                    op("dve", lambda v: v.tensor_tensor(out=prod[:], in0=qk_tok[:, 0, :, :], in1=qk_tok[:, 1, :, :], op=ALU.mult), r=[qktb], w=[prodb])
                    op("dve", lambda v: v.tensor_reduce(out=qkd, in_=prod[:], axis=AX.X, op=ALU.add), r=[prodb], w=[sm8sb])
                    op("dve", lambda v: v.tensor_tensor(out=prod[:], in0=qk_tok[:, 0, :, :], in1=n0t[:], op=ALU.mult), r=[qktb, n0b, prodb], w=[prodb])
                    op("dve", lambda v: v.tensor_reduce(out=qn, in_=prod[:], axis=AX.X, op=ALU.add), r=[prodb], w=[sm8sb])
                    op("dve", lambda v: v.tensor_tensor(out=s_, in0=qkd, in1=wi_t, op=ALU.mult), r=[sm8sb, gstb], w=[sm8sb])
                    op("dve", lambda v: v.tensor_tensor(out=den, in0=qn, in1=wst_t, op=ALU.mult), r=[sm8sb, gstb], w=[sm8sb])
                    op("dve", lambda v: v.tensor_tensor(out=den, in0=den, in1=s_, op=ALU.add), r=[sm8sb], w=[sm8sb])
                    op("dve", lambda v: v.tensor_tensor(out=vw[:], in0=v_tok[:], in1=wi_t[:, :, None].broadcast_to([NS, H, DV]), op=ALU.mult), r=[vtb, gstb], w=[vwb])
                    op("dve", lambda v: v.tensor_tensor(out=prod[:], in0=qk_tok[:, 1, :, :], in1=wi_t[:, :, None].broadcast_to([NS, H, DQK]), op=ALU.mult), r=[qktb, gstb, prodb], w=[prodb])
                    op("dve", lambda v: v.tensor_tensor(out=n0t[:], in0=n0t[:], in1=wst_t[:, :, None].broadcast_to([NS, H, DQK]), op=ALU.mult), r=[n0b, gstb], w=[n0b])
                    op("dve", lambda v: v.tensor_tensor(out=n0t[:], in0=n0t[:], in1=prod[:], op=ALU.add), r=[n0b, prodb], w=[n0b])
                    dma("sp", n_s, n0t[:], r=[n0b])
                    dma("sp", m_s, mt_t, r=[gstb])
                    for j in range(4):
                        Cj, Cjb = C0j[j % 2], C0jb[j % 2]
                        Cb, Cbb = C0bf[j % 2], ringB[j % 2]
                        for half in range(2):
                            po = half * 64
                            dma("sp", Cj[po:po + 64, :, :], st_C[:, 2 * j + half, :, :].rearrange("i d e -> d i e"), w=[Cjb])
                        op("act", lambda a, Cj=Cj, Cb=Cb: a.copy(out=Cb, in_=Cj[:]), r=[Cjb], w=[Cbb])
                        for half in range(2):
                            po = half * 64
                            hd = 2 * j + half

                            def mm_c(pe, po=po, j=j, Cb=Cb):
                                for nb in range(4):
                                    ins = pe.matmul(ps[0:NS, nb, :], lhsT=qkTs[po:po + 64, j, :], rhs=Cb[po:po + 64, nb * 4:(nb + 1) * 4, :].rearrange("p i e -> p (i e)"), start=True, stop=True)
                                return ins
                            op("pe", mm_c, r=[qkTsb, Cbb], w=[PB[0], PB[1], PB[2], PB[3]])
                            op("dve", lambda v: v.tensor_tensor(out=tmpd, in0=ps[0:NS, 0:4, :].rearrange("p b (i e) -> p (b i) e", e=DV), in1=ident[0:NS, 0:NS, None].broadcast_to([NS, NS, DV]), op=ALU.mult), r=[PB[0], PB[1], PB[2], PB[3], constb], w=[ringB[3]])
                            op("dve", lambda v, hd=hd: v.tensor_reduce(out=numC[:, hd, :], in_=tmpd.rearrange("p i e -> p e i"), axis=AX.X, op=ALU.add), r=[ringB[3]], w=[numCb])
                            op("dve", lambda v, hd=hd: v.tensor_tensor(out=Vbd, in0=vw[:, hd, None, :].broadcast_to([NS, NS, DV]), in1=ident[0:NS, 0:NS, None].broadcast_to([NS, NS, DV]), op=ALU.mult), r=[vwb, constb], w=[ringB[2]])

                            def mm_u(pe, po=po, hd=hd):
                                for nb in range(4):
                                    ins = pe.matmul(ps[po:po + 64, 4 + nb, :], lhsT=kbf[:, hd * 64:(hd + 1) * 64], rhs=Vbd[:, nb * 4:(nb + 1) * 4, :].rearrange("p i e -> p (i e)"), start=True, stop=True)
                                return ins
                            op("pe", mm_u, r=[kbfb, ringB[2]], w=[PB[4], PB[5], PB[6], PB[7]])
                        op("dve", lambda v, Cj=Cj, j=j: v.tensor_tensor(out=Cj[:], in0=Cj[:], in1=wstb[:, j, :, None].broadcast_to([128, NS, DV]), op=ALU.mult), r=[Cjb, wstbb], w=[Cjb])
                        op("dve", lambda v, Cj=Cj: v.tensor_tensor(out=Cj[:], in0=Cj[:], in1=ps[:, 4:8, :].rearrange("p b (i e) -> p (b i) e", e=DV), op=ALU.add), r=[Cjb, PB[4], PB[5], PB[6], PB[7]], w=[Cjb])
                        for half in range(2):
                            po = half * 64
                            dma("sp", C_s[:, 2 * j + half, :, :].rearrange("i d e -> d i e"), Cj[po:po + 64, :, :], r=[Cjb])
                    op("dve", lambda v: v.tensor_tensor(out=numC[:], in0=numC[:], in1=wst_t[:, :, None].broadcast_to([NS, H, DV]), op=ALU.mult), r=[numCb, gstb], w=[numCb])
                    op("dve", lambda v: v.tensor_tensor(out=vw[:], in0=v_tok[:], in1=s_[:, :, None].broadcast_to([NS, H, DV]), op=ALU.mult), r=[vtb, sm8sb, vwb], w=[vwb])
                    op("dve", lambda v: v.tensor_tensor(out=numC[:], in0=numC[:], in1=vw[:], op=ALU.add), r=[numCb, vwb], w=[numCb])
                    op("act", lambda a: a.activation(out=den, in_=den, func=AF.Abs), r=[sm8sb], w=[sm8sb])
                    ml_fin_a(NS, [(numC[:], 0, H, [numCb])], den, sm8sb, fls_t, gstb, so_tok[:], sotb, yts, ytsb, ybfs, ybfsb, sm8s, sm8sb)
                    ml_fin_b(NS, ybfs, ybfsb, T, [hyB[NBLK]])
                    tap("ml_ys", hy[:, 0, T:TT], [hyB[NBLK]])
                    all_engines_finish([vtb, sotb, qktb, kbfb, numCb, vwb, ytsb, n0b, prodb, sm8sb, wstbb, selb, Wdb, voTsb, qkTsb, gsrb, gstb, ybfsb] + C0jb + ringB + hyB)
                with ExitStack() as s2:
                    S = alloc_scratch(s2)
                    wout = sb("m_wout", [128, KC, D], BF16, stack=s2)
                    woutb = Buf()
                    out_ts = [sb(f"m_outt{i}", [128, KC, 512], stack=s2) for i in range(2)]
                    out_tb = [Buf(), Buf()]
                    zbt = [Buf() for _ in range(5)]
                    dma("pool", wout[:], ml_w_out[0].rearrange("(k p) n -> p k n", p=128), w=[woutb])
                    out_proj_update(sub, wout, woutb, lambda ti: hy[:, :, tiles[ti][0]:tiles[ti][0] + tiles[ti][1]], lambda ti: zbt[ti], [0, 1, 2, 3, 4], out_ts, out_tb, S)
                    all_engines_finish(scratch_bufs(S) + [woutb] + out_tb + zbt)

        STOP = taps.get("_stop", None)
        PROG = taps.get("_prog", "full")
        if PROG == "full":
            g0 = adaln_blocks(0, 7)
            steps(g0, 6)
            adaln_abg(0, [0])
            ffn(0, 0, 0, bg=g0, bg_n=1)
            steps(g0, 18)
            adaln_abg(0, [1, 2])
            mlstm_mixer(1)
            g1 = adaln_blocks(1, 7)
            ffn(0, 1, 2, bg=g1, bg_n=2)
            steps(g1, 18)
            adaln_abg(1, [0, 1, 2])
            ffn(1, 0, 0)
            conv_mixer(1)
            ffn(1, 1, 2)
        else:
            for step in PROG.split():
                if step[0] == "a":
                    g_ = adaln_blocks(int(step[1]), 7)
                    steps(g_, 18)
                    adaln_abg(int(step[1]), [0, 1, 2])
                elif step[0] == "f":
                    ffn(int(step[1]), int(step[2]), 0 if step[2] == "0" else 2)
                elif step[0] == "c":
                    conv_mixer(1)
                elif step[0] == "m":
                    mlstm_mixer(1)
        if K.stopped:
            K.stopped = False
            for e_ in ("pe", "act", "dve", "pool", "sp"):
                K.finish(e_, list(ALL_BUFS))

        with ExitStack() as s9:
            ys = [sb(f"ys{i}", [128, D], stack=s9) for i in range(2)]
            ysb = [Buf() for _ in range(2)]
            for blk in range(NBLK + 1):
                i = blk % 2
                n = 128 if blk < NBLK else NS
                xb = xB[blk // 4] if blk < NBLK else xB[4]
                pb = 2 * (blk % 4)

                def tr_y(pe, blk=blk, n=n, pb=pb):
                    for k in range(KC):
                        ins = pe.transpose(ps[0:n, pb + k // 4, (k % 4) * 128:(k % 4 + 1) * 128], xT[:, k, blk * 128:blk * 128 + n], ident[:])
                    return ins
                op("pe", tr_y, r=[xb, constb], w=[PB[pb], PB[pb + 1]])
                if blk % 2 == 0:
                    op("dve", lambda v, i=i, n=n, pb=pb: v.tensor_copy(out=ys[i][0:n, :], in_=ps[0:n, pb:pb + 2, :].rearrange("p b t -> p (b t)")), r=[PB[pb], PB[pb + 1]], w=[ysb[i]])
                else:
                    op("act", lambda a, i=i, n=n, pb=pb: a.copy(out=ys[i][0:n, :], in_=ps[0:n, pb:pb + 2, :].rearrange("p b t -> p (b t)")), r=[PB[pb], PB[pb + 1]], w=[ysb[i]])
                dst = y_p[blk * 128:(blk + 1) * 128, :] if blk < NBLK else y_s
                dma("sp", dst, ys[i][0:n, :], r=[ysb[i]])
            all_engines_finish(ysb)
        for q in ("sp", "pool"):
            for i, s in enumerate(K.dma_sems[q]):
                cntv = (K.dma_i[q] - i + len(K.dma_sems[q]) - 1) // len(K.dma_sems[q])
                if cntv > 0:
                    nc.sync.wait_ge(s, 16 * cntv)
    return nc


_PROG = {}


def _get_prog(taps=None):
    key = tuple(sorted((k, tuple(v) if isinstance(v, (list, tuple)) else v) for k, v in (taps or {}).items()))
    if key not in _PROG:
        _PROG[key] = build_program(taps)
    return _PROG[key]


def make_in_maps(inp):
    f = lambda a: np.ascontiguousarray(np.asarray(a, dtype=np.float32))
    shared = {k: f(inp[k]) for k in ("w_ada", "b_ada", "g_pre", "g_post", "ffn_wg", "ffn_wu", "ffn_wd", "ml_w_in", "ml_b_i",
                                      "ml_b_f", "ml_g_head", "ml_w_out", "cv_w_in", "cv_conv_w", "cv_w_out")}
    maps = []
    for c in range(NCORES):
        sl = slice(NS * c, NS * (c + 1))
        m = dict(shared)
        m["x_p"] = f(inp["x_prompt"][c])
        m["x_s"] = f(inp["x_sample"][sl, 0, :])
        m["c_all"] = f(np.concatenate([np.asarray(inp["c_sample"])[sl], np.asarray(inp["c_prompt"])[c:c + 1]], axis=0))
        m["st_C"] = f(inp["state_mlstm_C"][0, sl])
        m["st_n"] = f(inp["state_mlstm_n"][0, sl])
        m["st_m"] = f(inp["state_mlstm_m"][0, sl])
        m["st_cv"] = f(inp["state_conv"][0, sl])
        maps.append(m)
    return maps


def run(inp, taps=None):
    nc = _get_prog(taps)
    res = run_bass_kernel_spmd(nc, make_in_maps(inp), core_ids=list(range(NCORES)))
    return res.results


def kernel(**inp):
    R = run(inp)
    cat = lambda k: np.concatenate([r[k] for r in R], axis=0)
    y_prompt = np.stack([r["y_p"] for r in R], axis=0)
    y_sample = cat("y_s")[:, None, :]
    pC = np.stack([r["C_p"] for r in R], axis=0)[None]
    pn = np.stack([r["n_p"] for r in R], axis=0)[None]
    pm = np.stack([r["m_p"][:, 0] for r in R], axis=0)[None]
    pb = np.stack([r["cv_p"] for r in R], axis=0)[None]
    sC = cat("C_s")[None]
    sn = cat("n_s")[None]
    sm = cat("m_s")[None]
    sbuf_ = cat("cv_s")[None]
    return tuple(np.ascontiguousarray(a, dtype=np.float32) for a in (y_prompt, y_sample, pC, pn, pm, pb, sC, sn, sm, sbuf_))
```
